# Optimizing a Trainium2 kernel written in Bass

```python
import jax, jax.numpy as jnp
from jax import lax
import numpy as np

D_MODEL = 1024
BATCH = 32
SEQ = 256
DEPTH = 2
DEC_BATCH = 4
DEC_SEQ = 1024
PAST_LEN = 512

GRID_W = 64
HEAD_DIM = 64
N_HEADS = D_MODEL // HEAD_DIM
KV_HEADS = N_HEADS // 4
ATT_WIDTH = N_HEADS * HEAD_DIM
KV_WIDTH = KV_HEADS * HEAD_DIM
ROPE_FREQS = HEAD_DIM // 4
ROPE_THETA = 10000.0
Q_BLOCK = 128
GLA_HEADS = 4
GLA_DK = D_MODEL // 2 // GLA_HEADS
GLA_DV = D_MODEL // GLA_HEADS
GLA_QK_WIDTH = GLA_HEADS * GLA_DK
GLA_V_WIDTH = GLA_HEADS * GLA_DV
GATE_RANK = 16
GATE_NORM = 16.0
GLA_CHUNK = 64
N_KEYS = 128
N_EXPERTS = N_KEYS * N_KEYS
PEER_HEADS = 8
PEER_TOPK = 16
PEER_QDIM = 256
PEER_HALF = PEER_QDIM // 2
PEER_BLOCK = 128
ALPHA = (2.0 * DEPTH) ** 0.25
BETA = (8.0 * DEPTH) ** -0.25
LN_EPS = 1e-5
RMS_EPS = 1e-6
IN_SIZES = (ATT_WIDTH, KV_WIDTH, KV_WIDTH, GLA_QK_WIDTH, GLA_QK_WIDTH, GLA_V_WIDTH, GLA_V_WIDTH, 2 * GATE_RANK, 2 * D_MODEL)
IN_SPLITS = tuple(int(s) for s in np.cumsum(IN_SIZES)[:-1])
IN_WIDTH = int(sum(IN_SIZES))

kernel_name = 'hybrid_diffusion_gqa_gla_peer_step'


def rms_norm(x, g):
    xf = x.astype(jnp.float32)
    out = xf * lax.rsqrt(jnp.mean(xf * xf, axis=-1, keepdims=True) + RMS_EPS)
    return out.astype(x.dtype) * g


def layer_norm(x, g, b):
    xf = x.astype(jnp.float32)
    mu = jnp.mean(xf, axis=-1, keepdims=True)
    var = jnp.mean(jnp.square(xf - mu), axis=-1, keepdims=True)
    return ((xf - mu) * lax.rsqrt(var + LN_EPS)).astype(x.dtype) * g + b


def axial_rope(L, dtype):
    rows = L // GRID_W
    r = jnp.repeat(jnp.arange(rows, dtype=jnp.float32), GRID_W)
    col = jnp.tile(jnp.arange(GRID_W, dtype=jnp.float32), rows)
    inv = ROPE_THETA ** (-jnp.arange(ROPE_FREQS, dtype=jnp.float32) / ROPE_FREQS)
    ang = jnp.stack([r[:, None] * inv, col[:, None] * inv], axis=1)
    return jnp.cos(ang).astype(dtype), jnp.sin(ang).astype(dtype)


def apply_rope(x, cos, sin):
    B, L, H, _ = x.shape
    xr = x.reshape(B, L, H, 2, 2, ROPE_FREQS)
    x1, x2 = xr[..., 0, :], xr[..., 1, :]
    c, s = cos[None, :, None], sin[None, :, None]
    out = jnp.stack([x1 * c - x2 * s, x2 * c + x1 * s], axis=-2)
    return out.reshape(B, L, H, HEAD_DIM)


def gqa_attend(q, k, v):
    B, Lq, H, D = q.shape
    G = k.shape[2]
    R = H // G
    nb = Lq // Q_BLOCK
    qb = q.reshape(B, nb, Q_BLOCK, G, R, D).transpose(1, 0, 2, 3, 4, 5)

    def one_block(qblk):
        s = jnp.einsum('bqgrd,bkgd->bgrqk', qblk, k).astype(jnp.float32) * (D ** -0.5)
        p = jax.nn.softmax(s, axis=-1).astype(v.dtype)
        return jnp.einsum('bgrqk,bkgd->bqgrd', p, v)

    o = lax.map(one_block, qb)
    return o.transpose(1, 0, 2, 3, 4, 5).reshape(B, Lq, H * D)


def gla_scan(q, k, v, log_a, s0):
    B, L, H, _ = q.shape
    DV = v.shape[-1]
    n = L // GLA_CHUNK

    def chunks(t):
        return t.astype(jnp.float32).reshape(B, n, GLA_CHUNK, H, -1).transpose(1, 0, 3, 2, 4)

    qc, kc, vc, ac = chunks(q), chunks(k), chunks(v), chunks(log_a)
    bc = jnp.cumsum(ac, axis=3)
    mask = jnp.tril(jnp.ones((GLA_CHUNK, GLA_CHUNK), dtype=bool))

    def step(S, inp):
        qi, ki, vi, bi = inp
        bl = bi[:, :, -1:, :]
        q_e = qi * jnp.exp(bi)
        k_e = ki * jnp.exp(-bi)
        A = jnp.where(mask, jnp.einsum('bhid,bhjd->bhij', q_e, k_e), 0.0)
        o = jnp.einsum('bhij,bhjv->bhiv', A, vi) + jnp.einsum('bhid,bhdv->bhiv', q_e, S)
        S_new = jnp.exp(bl[:, :, 0, :])[..., None] * S + jnp.einsum('bhjd,bhjv->bhdv', ki * jnp.exp(bl - bi), vi)
        return S_new, o

    S, o = lax.scan(step, s0.astype(jnp.float32), (qc, kc, vc, bc))
    o = o.transpose(1, 0, 3, 2, 4).reshape(B, L, H, DV)
    return o.astype(v.dtype), S


def token_mixer(h, w_in, q_norm, k_norm, gate_w2, gate_b, gla_norm, w_attn_o, w_gla_o, w_out, ctx):
    B, L, _ = h.shape
    q, k, v, gq, gk, gv, gout, glr, gmerge = jnp.split(h @ w_in, IN_SPLITS, axis=-1)
    q = rms_norm(q.reshape(B, L, N_HEADS, HEAD_DIM), q_norm)
    k = rms_norm(k.reshape(B, L, KV_HEADS, HEAD_DIM), k_norm)
    v = v.reshape(B, L, KV_HEADS, HEAD_DIM)
    gq = gq.reshape(B, L, GLA_HEADS, GLA_DK) * (GLA_DK ** -0.5)
    gk = gk.reshape(B, L, GLA_HEADS, GLA_DK)
    gv = gv.reshape(B, L, GLA_HEADS, GLA_DV)
    z = jnp.einsum('bldr,drk->dblk', glr.reshape(B, L, 2, GATE_RANK), gate_w2) + gate_b[:, None, None, :]
    log_a = (jax.nn.log_sigmoid(z.astype(jnp.float32)) / GATE_NORM).reshape(2, B, L, GLA_HEADS, GLA_DK)
    if ctx is None:
        attn = gqa_attend(q, k, v)
        s0_f = jnp.zeros((B, GLA_HEADS, GLA_DK, GLA_DV), jnp.float32)
        s0_b = s0_f
    else:
        ctx_k, ctx_v, s0_f, s0_b = ctx
        cos, sin = axial_rope(L, h.dtype)
        q_r = apply_rope(q, cos, sin)
        k_r = apply_rope(k, cos, sin)
        attn = gqa_attend(q_r, jnp.concatenate([k_r, ctx_k], axis=1), jnp.concatenate([v, ctx_v], axis=1))
    o_f, s_f = gla_scan(gq, gk, gv, log_a[0], s0_f)
    o_b, s_b = gla_scan(jnp.flip(gq, 1), jnp.flip(gk, 1), jnp.flip(gv, 1), jnp.flip(log_a[1], 1), s0_b)
    o = o_f + jnp.flip(o_b, 1)
    o = (rms_norm(o, gla_norm) * jax.nn.silu(gout.reshape(B, L, GLA_HEADS, GLA_DV))).reshape(B, L, GLA_V_WIDTH)
    g_attn, g_gla = jnp.split(jax.nn.sigmoid(gmerge), 2, axis=-1)
    y = (g_attn * (attn @ w_attn_o) + g_gla * (o @ w_gla_o)) @ w_out
    if ctx is None:
        return y, (k, v, s_f, s_b)
    return y, None


def peer(h, wq, sub_keys, u, v):
    B, L, D = h.shape
    nb = (B * L) // PEER_BLOCK
    xb = h.reshape(nb, PEER_BLOCK, D)

    def one_block(xt):
        q = (xt @ wq).reshape(PEER_BLOCK, PEER_HEADS, 2, PEER_HALF)
        s = jnp.einsum('thpd,pnd->thpn', q, sub_keys).astype(jnp.float32)
        s1, i1 = lax.top_k(s[:, :, 0], PEER_TOPK)
        s2, i2 = lax.top_k(s[:, :, 1], PEER_TOPK)
        cand = (s1[..., :, None] + s2[..., None, :]).reshape(PEER_BLOCK, PEER_HEADS, PEER_TOPK * PEER_TOPK)
        sc, ci = lax.top_k(cand, PEER_TOPK)
        idx = jnp.take_along_axis(i1, ci // PEER_TOPK, axis=-1) * N_KEYS + jnp.take_along_axis(i2, ci % PEER_TOPK, axis=-1)
        g = jax.nn.softmax(sc, axis=-1)
        a = jax.nn.gelu(jnp.einsum('td,thkd->thk', xt, u[idx]).astype(jnp.float32))
        return jnp.einsum('thk,thkd->td', (g * a).astype(xt.dtype), v[idx])

    return lax.map(one_block, xb).reshape(B, L, D)


def trunk_layer(x, cond, lp, ctx):
    (ada_w, ada_b, w_in, q_norm, k_norm, gate_w2, gate_b, gla_norm, w_attn_o, w_gla_o, w_out,
     ln1_g, ln1_b, ln2_g, ln2_b, peer_wq, peer_sub_keys, peer_u, peer_v) = lp
    mod = (jax.nn.silu(cond) @ ada_w + ada_b).reshape(-1, 1, 6 * D_MODEL)
    shift1, scale1, gate1, shift2, scale2, gate2 = jnp.split(mod, 6, axis=-1)
    h = x * (1.0 + scale1) + shift1
    mix, ctx_out = token_mixer(h, w_in, q_norm, k_norm, gate_w2, gate_b, gla_norm, w_attn_o, w_gla_o, w_out, ctx)
    x = layer_norm(ALPHA * x + gate1 * mix, ln1_g, ln1_b)
    h = x * (1.0 + scale2) + shift2
    x = layer_norm(ALPHA * x + gate2 * peer(h, peer_wq, peer_sub_keys, peer_u, peer_v), ln2_g, ln2_b)
    return x, ctx_out


def setup_inputs(seed: int = 0) -> dict:
    key = jax.random.key(seed)
    ks = jax.random.split(key, 32)
    f32 = jnp.float32
    nrm = lambda k, shape, scale: jax.random.normal(k, shape, f32) * scale
    return {
        'x_prompt': nrm(ks[0], (BATCH, SEQ, D_MODEL), 1.0),
        'x_sample': nrm(ks[1], (DEC_BATCH, DEC_SEQ, D_MODEL), 1.0),
        'cache_k': nrm(ks[2], (DEC_BATCH, DEPTH, PAST_LEN, KV_HEADS, HEAD_DIM), 1.0),
        'cache_v': nrm(ks[3], (DEC_BATCH, DEPTH, PAST_LEN, KV_HEADS, HEAD_DIM), 1.0),
        'state_gla': nrm(ks[4], (DEC_BATCH, DEPTH, 2, GLA_HEADS, GLA_DK, GLA_DV), 0.5),
        'c': nrm(ks[5], (DEC_BATCH, D_MODEL), 1.0),
        'c_ctx': nrm(ks[6], (D_MODEL,), 1.0),
        'ada_w': nrm(ks[7], (DEPTH, D_MODEL, 6 * D_MODEL), D_MODEL ** -0.5),
        'ada_b': nrm(ks[8], (DEPTH, 6 * D_MODEL), 0.02),
        'w_in': nrm(ks[9], (DEPTH, D_MODEL, IN_WIDTH), D_MODEL ** -0.5),
        'q_norm': 1.0 + nrm(ks[10], (DEPTH, HEAD_DIM), 0.02),
        'k_norm': 1.0 + nrm(ks[11], (DEPTH, HEAD_DIM), 0.02),
        'gate_w2': nrm(ks[12], (DEPTH, 2, GATE_RANK, GLA_QK_WIDTH), GATE_RANK ** -0.5),
        'gate_b': jax.random.uniform(ks[13], (DEPTH, 2, GLA_QK_WIDTH), f32, 1.0, 4.0),
        'gla_norm': 1.0 + nrm(ks[14], (DEPTH, GLA_DV), 0.02),
        'w_attn_o': nrm(ks[15], (DEPTH, ATT_WIDTH, D_MODEL), ATT_WIDTH ** -0.5),
        'w_gla_o': nrm(ks[16], (DEPTH, GLA_V_WIDTH, D_MODEL), GLA_V_WIDTH ** -0.5),
        'w_out': nrm(ks[17], (DEPTH, D_MODEL, D_MODEL), BETA * D_MODEL ** -0.5),
        'ln1_g': 1.0 + nrm(ks[18], (DEPTH, D_MODEL), 0.02),
        'ln1_b': nrm(ks[19], (DEPTH, D_MODEL), 0.02),
        'ln2_g': 1.0 + nrm(ks[20], (DEPTH, D_MODEL), 0.02),
        'ln2_b': nrm(ks[21], (DEPTH, D_MODEL), 0.02),
        'peer_wq': nrm(ks[22], (DEPTH, D_MODEL, PEER_HEADS * PEER_QDIM), D_MODEL ** -0.5),
        'peer_sub_keys': nrm(ks[23], (DEPTH, 2, N_KEYS, PEER_HALF), PEER_HALF ** -0.5),
        'peer_u': nrm(ks[24], (DEPTH, N_EXPERTS, D_MODEL), D_MODEL ** -0.5),
        'peer_v': nrm(ks[25], (DEPTH, N_EXPERTS, D_MODEL), BETA * PEER_HEADS ** -0.5),
    }


def reference(x_prompt, x_sample, cache_k, cache_v, state_gla, c, c_ctx, ada_w, ada_b, w_in, q_norm, k_norm,
              gate_w2, gate_b, gla_norm, w_attn_o, w_gla_o, w_out, ln1_g, ln1_b, ln2_g, ln2_b,
              peer_wq, peer_sub_keys, peer_u, peer_v):
    params = (ada_w, ada_b, w_in, q_norm, k_norm, gate_w2, gate_b, gla_norm, w_attn_o, w_gla_o, w_out,
              ln1_g, ln1_b, ln2_g, ln2_b, peer_wq, peer_sub_keys, peer_u, peer_v)
    xp = x_prompt
    ks, vs, ss = [], [], []
    for l in range(DEPTH):
        lp = tuple(p[l] for p in params)
        xp, (k_l, v_l, sf_l, sb_l) = trunk_layer(xp, c_ctx, lp, None)
        ks.append(k_l)
        vs.append(v_l)
        ss.append(jnp.stack([sf_l, sb_l], axis=1))
    new_cache_k = jnp.stack(ks, axis=1)
    new_cache_v = jnp.stack(vs, axis=1)
    new_state_gla = jnp.stack(ss, axis=1)
    xs = x_sample
    for l in range(DEPTH):
        lp = tuple(p[l] for p in params)
        ctx = (cache_k[:, l], cache_v[:, l], state_gla[:, l, 0], state_gla[:, l, 1])
        xs, _ = trunk_layer(xs, c, lp, ctx)
    return (xp, xs, new_cache_k, new_cache_v, new_state_gla)
```

```python
import os
import numpy as np
from contextlib import ExitStack
import concourse.bass as bass
import concourse.mybir as mybir
from concourse.bass_utils import run_bass_kernel_spmd

F32 = mybir.dt.float32
U32 = mybir.dt.uint32
I32 = mybir.dt.int32
AF = mybir.ActivationFunctionType
ALU = mybir.AluOpType
AX = mybir.AxisListType

DEPTH = 2
ALPHA = (2.0 * DEPTH) ** 0.25
LN_EPS = 1e-5
RMS_EPS = 1e-6
NEG = -3.0e38


class Buf:
    __slots__ = ("ap", "name", "lw", "rd", "sem", "cnt", "psum", "base")

    def __init__(self, ap, name="", psum=False, base=None):
        self.ap = ap
        self.name = name
        self.psum = psum
        self.base = base
        self.lw = None
        self.rd = []
        self.sem = None
        self.cnt = 0

    def __getitem__(self, k):
        return self.ap[k]


class SwUse:
    __slots__ = ("sem", "uid")

    def __init__(self, sem, uid):
        self.sem = sem
        self.uid = uid


class FW:
    ENG = ("pe", "act", "dve", "pool", "sp")

    def __init__(self, nc, stack):
        self.nc = nc
        self.stack = stack
        self.eng = {"pe": nc.tensor, "act": nc.scalar, "dve": nc.vector, "pool": nc.gpsimd, "sp": nc.sync}
        self.sem = {e: stack.enter_context(nc.semaphore("s_" + e)) for e in self.ENG}
        self.cnt = {e: 0 for e in self.ENG}
        self.waited = {e: {} for e in self.ENG}
        self.ops = {e: [] for e in self.ENG}
        self.nsem = 0
        self.dmabufs = []
        self.free_sems = []
        self.sw_sems = {}
        self.sw_uses = 0
        self.all_sems = []

    def sb(self, name, shape, dt=F32):
        return self.stack.enter_context(self.nc.sbuf_tensor(name, list(shape), dt))

    def ps(self, name, shape, dt=F32):
        return self.stack.enter_context(self.nc.psum_tensor(name, list(shape), dt))

    def _dsem(self, b):
        if b.sem is None:
            if self.free_sems:
                b.sem = self.free_sems.pop()
            else:
                h = self.stack.enter_context(self.nc.semaphore("d%d" % self.nsem))
                self.nsem += 1
                b.sem = [h, 0]
                self.all_sems.append(b.sem)
            self.dmabufs.append(b)
        return b.sem

    def _deps(self, e, reads, writes):
        deps = []
        for r in reads:
            if r.lw is not None:
                deps.append(r.lw)
            if r.psum:
                deps.extend(d for d in r.rd if d[0] != e)
        for w in writes:
            if w.lw is not None:
                deps.append(w.lw)
            deps.extend(w.rd)
        need = {}
        for d in deps:
            key = d[0]
            if isinstance(key, str) and key == e and e == "pe":
                continue
            if need.get(key, 0) < d[1]:
                need[key] = d[1]
        out = []
        wd = self.waited[e]
        for key, v in need.items():
            if wd.get(key, 0) >= v:
                continue
            wd[key] = v
            out.append((key, v))
        return out

    def _emit_waits(self, e, waits):
        eng = self.eng[e]
        for key, v in waits:
            s = self.sem[key] if isinstance(key, str) else (key.sem if isinstance(key, SwUse) else key)
            self.ops[e].append((lambda eng=eng, s=s, v=v: eng.wait_ge(s, v)))

    @staticmethod
    def _norm(bufs):
        out = []
        for b in bufs:
            while b.base is not None:
                b = b.base
            out.append(b)
        return out

    def op(self, e, fn, reads=(), writes=()):
        reads = self._norm(reads); writes = self._norm(writes)
        waits = self._deps(e, reads, writes)
        self._emit_waits(e, waits)
        self.cnt[e] += 1
        c = self.cnt[e]
        s = self.sem[e]
        eng = self.eng[e]
        self.ops[e].append((lambda eng=eng, fn=fn, s=s: fn(eng).then_inc(s, 1)))
        dep = (e, c)
        for r in reads:
            r.rd.append(dep)
        for w in writes:
            w.lw = dep
            w.rd = []
        return dep

    def swdma(self, fn, sb_buf, extra_reads=()):
        q = "pool"
        reads = self._norm(list(extra_reads))
        writes = [sb_buf]
        waits = self._deps(q, reads, writes)
        self._emit_waits(q, waits)
        if sb_buf.name not in self.sw_sems:
            self.sw_sems[sb_buf.name] = [self.stack.enter_context(self.nc.semaphore("g%d" % len(self.sw_sems))), 0]
        sp_ = self.sw_sems[sb_buf.name]
        sp_[1] += 16
        s, v = sp_[0], sp_[1]
        eng = self.eng[q]
        self.ops[q].append((lambda eng=eng, fn=fn, s=s: fn(eng).then_inc(s, 16)))
        dep = (s, v)
        for r in reads:
            r.rd.append(dep)
        sb_buf.lw = dep
        sb_buf.rd = []
        return dep

    def dma(self, q, fn, sb_buf, is_load, extra_reads=(), extra_writes=()):
        reads = list(extra_reads) + ([] if is_load else [sb_buf])
        writes = list(extra_writes) + ([sb_buf] if is_load else [])
        waits = self._deps(q, reads, writes)
        self._emit_waits(q, waits)
        sp_ = self._dsem(sb_buf)
        sp_[1] += 16
        v = sp_[1]
        s = sp_[0]
        eng = self.eng[q]
        self.ops[q].append((lambda eng=eng, fn=fn, s=s: fn(eng).then_inc(s, 16)))
        dep = (s, v)
        for r in reads:
            r.rd.append(dep)
        for w in writes:
            w.lw = dep
            w.rd = []
        return dep

    def barrier(self):
        waits = []
        wd = self.waited["sp"]
        for e in ("pe", "act", "dve", "pool"):
            if self.cnt[e] > wd.get(e, 0):
                wd[e] = self.cnt[e]
                waits.append((e, self.cnt[e]))
        for sp_ in self.all_sems:
            if sp_[1] > wd.get(sp_[0], 0):
                wd[sp_[0]] = sp_[1]
                waits.append((sp_[0], sp_[1]))
        for b in self.dmabufs:
            self.free_sems.append(b.sem)
            b.sem = None
        self.dmabufs = []
        self._emit_waits("sp", waits)
        self.cnt["sp"] += 1
        c = self.cnt["sp"]
        s = self.sem["sp"]
        eng = self.eng["sp"]
        self.ops["sp"].append((lambda eng=eng, s=s: eng.nop().then_inc(s, 1)))
        for e in ("pe", "act", "dve", "pool"):
            self.waited[e]["sp"] = c
            self._emit_waits(e, [("sp", c)])

    def run_block(self):
        nc = self.nc
        with nc.Block() as block:
            @block.sync
            def _(e):
                for f in self.ops["sp"]:
                    f()

            @block.tensor
            def _(e):
                for f in self.ops["pe"]:
                    f()

            @block.scalar
            def _(e):
                for f in self.ops["act"]:
                    f()

            @block.vector
            def _(e):
                for f in self.ops["dve"]:
                    f()

            @block.gpsimd
            def _(e):
                for f in self.ops["pool"]:
                    f()


class Arena:
    def __init__(self, t, size):
        self.t = t
        self.size = size
        self.off = 0

    def reset(self):
        self.off = 0

    def take(self, n, parts=128):
        assert self.off + n <= self.size, ("arena overflow", self.off, n, self.size)
        ap = self.t[0:parts, self.off:self.off + n]
        self.off += n
        return ap


def build_program(n_ctx=4, do_sample=True, depth=DEPTH, do_peer=True, dbg=None):
    nc = bass.Bass("TRN2", target_bir_lowering=False)

    def DI(name, shape, dt=F32):
        return nc.dram_tensor(name, list(shape), dt, kind="ExternalInput").ap()

    def DO(name, shape, dt=F32):
        return nc.dram_tensor(name, list(shape), dt, kind="ExternalOutput").ap()

    xc_d = DI("xc", [4, 256, 1024]); xs_d = DI("xs", [1024, 1024])
    ck_d = DI("ck", [2, 512, 256]); cv_d = DI("cv", [2, 512, 256])
    sg_d = DI("sg", [2, 2, 4, 128, 256])
    condT_d = DI("condT", [128, 8, 2])
    ada_w_d = DI("ada_w_t", [2, 24, 128, 2048]); ada_bT_d = DI("ada_bT", [128, 2, 48])
    w_in_d = DI("w_in_t", [2, 26, 128, 2048])
    qn_d = DI("q_norm", [2, 64]); kn_d = DI("k_norm", [2, 64])
    gw2b_c_d = DI("gw2b_c", [2, 17, 2, 512]); gw2b_s_d = DI("gw2b_s", [2, 17, 2, 512])
    wglr_c_d = DI("wglr_c", [2, 1024, 32]); wglr_s_d = DI("wglr_s", [2, 1024, 32])
    glan_d = DI("gla_norm", [2, 256])
    wao_d = DI("w_attn_o_t", [2, 4, 128, 2048]); wgo_d = DI("w_gla_o_t", [2, 4, 128, 2048]); wout_d = DI("w_out_t", [2, 4, 128, 2048])
    ln1g_d = DI("ln1_g", [2, 1024]); ln1b_d = DI("ln1_b", [2, 1024])
    ln2g_d = DI("ln2_g", [2, 1024]); ln2b_d = DI("ln2_b", [2, 1024])
    wq_d = DI("peer_wq_t", [2, 8, 128, 2048]); sk_d = DI("peer_sub_keys", [2, 2, 128, 128])
    puv_d = [DI("puv0", [16384, 2048]), DI("puv1", [16384, 2048])]
    ident_d = DI("c_ident", [128, 128])
    tri_d = DI("c_tri", [128, 6, 128])
    cind_d = DI("c_cind", [128, 2])
    rope_d = DI("c_rope", [128, 8, 2, 32])
    iota_d = DI("c_iota", [128, 16])

    yc_d = DO("yc", [4, 256, 1024]); ys_d = DO("ys", [512, 1024])
    nk_d = DO("nk", [4, 2, 256, 256]); nv_d = DO("nv", [4, 2, 256, 256])
    ns_d = DO("ns", [4, 2, 2, 4, 128, 256])

    with ExitStack() as st:
        fw = FW(nc, st)
        op = fw.op

        def B(ap, name=""):
            return Buf(ap, name)

        ident = B(fw.sb("ident", [128, 128]))
        tri = B(fw.sb("tri", [128, 6, 128]))
        cind = B(fw.sb("cind", [128, 2]))
        rope = B(fw.sb("rope", [128, 8, 2, 32]))
        iota = B(fw.sb("iota", [128, 16]))
        modT = B(fw.sb("modT", [128, 2, 2, 48]))
        adab = B(fw.sb("adab", [128, 2, 48]))
        scond = B(fw.sb("scond", [128, 8, 2]))
        gate_bc = B(fw.sb("gate_bc", [128, 1024]))
        lng = B(fw.sb("lng", [128, 1024])); lnb = B(fw.sb("lnb", [128, 1024]))
        qn_bc = B(fw.sb("qn_bc", [128, 64])); kn_bc = B(fw.sb("kn_bc", [128, 64]))
        glan_bc = B(fw.sb("glan_bc", [128, 256]))
        gw2b = B(fw.sb("gw2b_sb", [17, 2, 512]))
        skT = B(fw.sb("skT", [128, 2, 128]))
        xall = fw.sb("xall", [128, 8, 1024])
        xt = [B(xall[:, i, :], "x%d" % i) for i in range(8)]
        hTall = fw.sb("hTall", [128, 2, 8, 128])
        hT = [B(hTall[:, i, :, :], "hT%d" % i) for i in range(2)]
        NW = int(os.environ.get("NW", "3"))
        Wt = [B(fw.sb("W%d" % i, [128, 8, 256]), "W%d" % i) for i in range(NW)]
        w_ctr = [0]

        def nextW():
            w = Wt[w_ctr[0] % NW]
            w_ctr[0] += 1
            return w
        Wg = B(fw.sb("Wg", [128, 8, 32]))
        ARENA_N = int(os.environ.get("ARENA_N", "28000"))
        arena_t = fw.sb("arena", [128, ARENA_N])
        arena = Arena(arena_t, ARENA_N)
        psb = [Buf(fw.ps("ps%d" % i, [128, 512]), "ps%d" % i, psum=True) for i in range(8)]
        ps_ctr = [0]

        def PS():
            b = psb[ps_ctr[0] % 6]
            ps_ctr[0] += 1
            return b

        def load(q, buf, out_ap, in_ap):
            fw.dma(q, lambda e, o=out_ap, i=in_ap: e.dma_start(out=o, in_=i), buf, True)

        def store(q, buf, out_ap, in_ap):
            fw.dma(q, lambda e, o=out_ap, i=in_ap: e.dma_start(out=o, in_=i), buf, False)

        def mm(ob, o, lb, l, rb, r, start, stop, skip=False):
            op("pe", lambda e, o=o, l=l, r=r, s0=start, s1=stop, sk=skip: e.matmul(o, l, r, start=s0, stop=s1, skip_group_check=sk),
               [lb, rb], [ob])

        def tr(ob, o, ib, i):
            op("pe", lambda e, o=o, i=i: e.transpose(o, i, ident[:]), [ib, ident], [ob])

        def act(ob, o, ib, i, func, scale=1.0, bias=0.0, accum=None, accb=None, extra=()):
            if accum is None:
                op("act", lambda e, o=o, i=i, f=func, s=scale, b=bias: e.activation(out=o, in_=i, func=f, scale=s, bias=b),
                   [ib] + list(extra), [ob])
            else:
                op("act", lambda e, o=o, i=i, f=func, s=scale, b=bias, a=accum: e.activation(out=o, in_=i, func=f, scale=s, bias=b, accum_out=a),
                   [ib] + list(extra), [ob, accb])

        def tt(eng, ob, o, ab, a, bb, b, alu):
            op(eng, lambda e, o=o, a=a, b=b, alu=alu: e.tensor_tensor(out=o, in0=a, in1=b, op=alu), [ab, bb], [ob])

        def cp(eng, ob, o, ib, i):
            if eng == "act":
                op("act", lambda e, o=o, i=i: e.copy(out=o, in_=i), [ib], [ob])
            else:
                op(eng, lambda e, o=o, i=i: e.tensor_copy(out=o, in_=i), [ib], [ob])

        def ts(ob, o, ib, i, s1, s2, op0, op1, extra=()):
            op("dve", lambda e, o=o, i=i, s1=s1, s2=s2, op0=op0, op1=op1: e.tensor_scalar(out=o, in0=i, scalar1=s1, scalar2=s2, op0=op0, op1=op1),
               [ib] + list(extra), [ob])

        def stt(ob, o, ab, a, sc, bb, b, op0, op1, extra=(), accum=None, accb=None):
            if accum is None:
                op("dve", lambda e, o=o, a=a, sc=sc, b=b, op0=op0, op1=op1: e.scalar_tensor_tensor(out=o, in0=a, scalar=sc, in1=b, op0=op0, op1=op1),
                   [ab, bb] + list(extra), [ob])
            else:
                op("dve", lambda e, o=o, a=a, sc=sc, b=b, op0=op0, op1=op1, ac=accum: e.scalar_tensor_tensor(out=o, in0=a, scalar=sc, in1=b, op0=op0, op1=op1, accum_out=ac),
                   [ab, bb] + list(extra), [ob, accb])

        def memset(eng, ob, o, val):
            op(eng, lambda e, o=o, v=val: e.memset(o, v), [], [ob])

        def rstd_from(ssb, ss_ap, outb, out_ap, inv_n, eps):
            ts(outb, out_ap, ssb, ss_ap, inv_n, eps, ALU.mult, ALU.add)
            act(outb, out_ap, outb, out_ap, AF.Sqrt)
            op("dve", lambda e, o=out_ap: e.reciprocal(out=o, in_=o), [outb], [outb])

        load("sp", ident, ident[:], ident_d)
        load("sp", tri, tri[:], tri_d)
        load("sp", cind, cind[:], cind_d)
        load("sp", rope, rope[:], rope_d)
        load("sp", iota, iota[:], iota_d)
        load("sp", scond, scond[:], condT_d)
        load("sp", adab, adab[:], ada_bT_d)
        act(scond, scond[:], scond, scond[:], AF.Silu)

        if dbg == -1:
            store("sp", scond, yc_d[0, 0:128, 0:16], scond[:].rearrange("p c j -> p (c j)"))
            fw.barrier(); fw.run_block()
            return nc
        for l in range(depth):
            pm = PS()
            for blk in range(int(os.environ.get("NBLK", "24"))):
                w = nextW()
                load("sp", w, w[:], ada_w_d[l, blk].rearrange("p (c n) -> p c n", c=8))
                for c in range(2):
                    cg = blk * 2 + c
                    if os.environ.get("MODV") == "1":
                        continue
                    for kc in range(8):
                        mm(pm, pm[:, cg * 2:cg * 2 + 2], w, w[:, kc, c * 128:(c + 1) * 128], scond, scond[:, kc, :], kc == 0, kc == 7)
            for j in range(2):
                tt("dve", modT, modT[:, l, j, :], pm, pm[:, 0:96].rearrange("p (c j) -> p c j", j=2)[:, :, j], adab, adab[:, l, :], ALU.add)
            for j in range(2):
                for c0 in (8, 32):
                    ts(modT, modT[:, l, j, c0:c0 + 8], modT, modT[:, l, j, c0:c0 + 8], 1.0, None, ALU.add, ALU.bypass)

        if dbg == 0:
            store("sp", modT, yc_d[0, 0:128, 0:192], modT[:].rearrange("p l j c -> p (l j c)"))
            fw.barrier(); fw.run_block()
            return nc
        def make_hT(xb, slot, l, job, which):
            sh0 = 0 if which == 1 else 24
            sc0 = 8 if which == 1 else 32
            h = hT[slot]
            for half in range(2):
                p = PS()
                for c in range(4):
                    cc = half * 4 + c
                    tr(p, p[:, c * 128:(c + 1) * 128], xb, xb[:, cc * 128:(cc + 1) * 128])
                sc_b = modT[:, l, job, sc0 + half * 4:sc0 + half * 4 + 4].unsqueeze(2).to_broadcast([128, 4, 128])
                sh_b = modT[:, l, job, sh0 + half * 4:sh0 + half * 4 + 4].unsqueeze(2).to_broadcast([128, 4, 128])
                tt("dve", h, h[:, half * 4:half * 4 + 4, :], p, p[:].rearrange("p (c t) -> p c t", c=4), modT, sc_b, ALU.mult)
                tt("dve", h, h[:, half * 4:half * 4 + 4, :], h, h[:, half * 4:half * 4 + 4, :], modT, sh_b, ALU.add)

        def proj(slot, w, ncols, pb, pout, wcol0=0):
            for kc in range(8):
                mm(pb, pout, hT[slot], hT[slot][:, kc, :], w, w[:, kc, wcol0:wcol0 + ncols], kc == 0, kc == 7)

        def wload(w, packed_l, j):
            load("sp", w, w[:], packed_l[j].rearrange("p (c n) -> p c n", c=8))

        def gate_bcast(l, job, c0):
            gB = B(arena.take(1024).rearrange("p (c m) -> p c m", c=8), "gB")
            cp("dve", gB, gB[:], modT, modT[:, l, job, c0:c0 + 8].unsqueeze(2).to_broadcast([128, 8, 128]))
            for half in range(2):
                p = PS()
                for c in range(4):
                    mm(p, p[:, c * 128:(c + 1) * 128], gB, gB[:, half * 4 + c, :], ident, ident[:], True, True)
                cp("act", gate_bc, gate_bc[:, half * 512:(half + 1) * 512], p, p[:])

        def layer_norm(preb, pre_ap, outb, out_ap, small):
            st6 = small["st6"]; mv = small["mv"]; rs = small["rs"]
            for c in range(2):
                op("dve", lambda e, o=st6[:, c * 6:(c + 1) * 6], i=pre_ap[:, c * 512:(c + 1) * 512]: e.bn_stats(out=o, in_=i), [preb], [st6])
            op("dve", lambda e: e.bn_aggr(out=mv[:], in_=st6[:]), [st6], [mv])
            ts(rs, rs[:], mv, mv[:, 1:2], 1.0, LN_EPS, ALU.mult, ALU.add)
            act(rs, rs[:], rs, rs[:], AF.Sqrt)
            op("dve", lambda e: e.reciprocal(out=rs[:], in_=rs[:]), [rs], [rs])
            ts(outb, out_ap, preb, pre_ap, mv[:, 0:1], rs[:, 0:1], ALU.subtract, ALU.mult, extra=[mv, rs])
            tt("dve", outb, out_ap, outb, out_ap, lng, lng[:], ALU.mult)
            tt("dve", outb, out_ap, outb, out_ap, lnb, lnb[:], ALU.add)

        def gstop(n):
            if os.environ.get("GSTOP") == str(n):
                raise StopIteration

        def run_unit(is_sample, useq):
            if not is_sample and not isinstance(useq, (list, tuple)):
                useq = [useq]
            NT = 8 if is_sample else 2 * len(useq)
            NKC = 12 if is_sample else 2
            job = 1 if is_sample else 0
            y_dst = ys_d if is_sample else yc_d[useq[0]]

            def xrow(t):
                if is_sample:
                    return xs_d[t * 128:(t + 1) * 128, :]
                return xc_d[useq[t // 2]][(t % 2) * 128:(t % 2 + 1) * 128, :]

            def yrow(t):
                if is_sample:
                    return ys_d[t * 128:(t + 1) * 128, :]
                return yc_d[useq[t // 2]][(t % 2) * 128:(t % 2 + 1) * 128, :]
            for l in range(depth):
                win = w_in_d[l]
                last_half = is_sample and (l == depth - 1)
                NOWN = NT // 2 if last_half else NT
                fw.barrier()
                arena.reset()
                kT = B(arena.take(2 * 1536).rearrange("p (j k) -> p j k", j=2), "kT")
                vA = B(arena.take(12 * 4 * 65).rearrange("p (c g d) -> p c g d", c=12, g=4), "vA")
                o_all = [B(arena.take(1024), "o%d" % i) for i in range(NT)]
                glr_sb = B(arena.take(NT * 32).rearrange("p (t c) -> p t c", c=32), "glr")
                small = {"st6": B(arena.take(12)), "mv": B(arena.take(2)), "rs": B(arena.take(1))}
                mark_persist = arena.off
                load("sp", qn_bc, qn_bc[:], qn_d[l:l + 1, :].partition_broadcast(128) if False else qn_d[l].partition_broadcast(128))
                load("sp", kn_bc, kn_bc[:], kn_d[l].partition_broadcast(128))
                load("sp", glan_bc, glan_bc[:], glan_d[l].partition_broadcast(128))
                load("sp", gw2b, gw2b[:], (gw2b_s_d if is_sample else gw2b_c_d)[l])
                load("sp", lng, lng[:], ln1g_d[l].partition_broadcast(128))
                load("sp", lnb, lnb[:], ln1b_d[l].partition_broadcast(128))
                memset("dve", vA, vA[:], 1.0)

                wk_ = nextW(); wv2_ = nextW()
                wload(wk_, win, 4); wload(wv2_, win, 5)
                load("sp", Wg, Wg[:], (wglr_s_d if is_sample else wglr_c_d)[l].rearrange("(c p) n -> p c n", p=128))
                t_sq = B(arena.take(256)); t_ss = B(arena.take(4)); t_kn = B(arena.take(256)); t_kr = B(arena.take(256))
                t_a = B(arena.take(128)); t_b = B(arena.take(128)); t_v = B(arena.take(256)); t_kp = B(arena.take(256))
                for t in range(NT):
                    if l == 0:
                        load("sp", xt[t], xt[t][:], xrow(t))
                    make_hT(xt[t], 0, l, job, 1)
                    pk = PS(); proj(0, wk_, 256, pk, pk[:, 0:256])
                    pv = PS(); proj(0, wv2_, 256, pv, pv[:, 0:256])
                    for kc in range(8):
                        mm(pv, pv[:, 256:288], hT[0], hT[0][:, kc, :], Wg, Wg[:, kc, :], kc == 0, kc == 7)
                    act(t_sq, t_sq[:], pk, pk[:, 0:256], AF.Square)
                    op("dve", lambda e: e.tensor_reduce(out=t_ss[:], in_=t_sq[:].rearrange("p (h d) -> p h d", d=64), op=ALU.add, axis=AX.X), [t_sq], [t_ss])
                    rstd_from(t_ss, t_ss[:], t_ss, t_ss[:], 1.0 / 64, RMS_EPS)
                    tt("dve", t_kn, t_kn[:].rearrange("p (h d) -> p h d", d=64), pk, pk[:, 0:256].rearrange("p (h d) -> p h d", d=64),
                       t_ss, t_ss[:].unsqueeze(2).to_broadcast([128, 4, 64]), ALU.mult)
                    tt("dve", t_kn, t_kn[:].rearrange("p (h d) -> p h d", d=64), t_kn, t_kn[:].rearrange("p (h d) -> p h d", d=64),
                       kn_bc, kn_bc[:].unsqueeze(1).to_broadcast([128, 4, 64]), ALU.mult)
                    cp("act", t_v, t_v[:], pv, pv[:, 0:256])
                    cp("act", glr_sb, glr_sb[:, t, :], pv, pv[:, 256:288])
                    if not is_sample:
                        store("sp", t_kn, nk_d[useq[t // 2], l, (t % 2) * 128:(t % 2 + 1) * 128, :], t_kn[:])
                        store("sp", t_v, nv_d[useq[t // 2], l, (t % 2) * 128:(t % 2 + 1) * 128, :], t_v[:])
                        ksrc = t_kn
                    else:
                        do_rope(t_kn, t_kr, 4, t, t_a, t_b)
                        ksrc = t_kr
                    cp("dve", vA, vA[:, t, :, 0:64], t_v, t_v[:].rearrange("p (g d) -> p g d", d=64))
                    cp("dve", t_kp, t_kp[:].rearrange("p (j hi d) -> p hi j d", j=2, hi=2), ksrc, ksrc[:].rearrange("p (hi j d) -> p hi j d", hi=2, j=2))
                    pt = PS()
                    for j in range(2):
                        tr(pt, pt[:, j * 128:(j + 1) * 128], t_kp, t_kp[:, j * 128:(j + 1) * 128])
                        cp("act", kT, kT[:, j, t * 128:(t + 1) * 128], pt, pt[:, j * 128:(j + 1) * 128])
                if is_sample:
                    for i in range(4):
                        load("sp", t_kr, t_kr[:], ck_d[l, i * 128:(i + 1) * 128, :])
                        cp("dve", t_kp, t_kp[:].rearrange("p (j hi d) -> p hi j d", j=2, hi=2), t_kr, t_kr[:].rearrange("p (hi j d) -> p hi j d", hi=2, j=2))
                        pt = PS()
                        for j in range(2):
                            tr(pt, pt[:, j * 128:(j + 1) * 128], t_kp, t_kp[:, j * 128:(j + 1) * 128])
                            cp("act", kT, kT[:, j, 1024 + i * 128:1024 + (i + 1) * 128], pt, pt[:, j * 128:(j + 1) * 128])
                        load("sp", vA, vA[:, 8 + i, :, 0:64], cv_d[l, i * 128:(i + 1) * 128, :].rearrange("p (g d) -> p g d", d=64))

                if dbg == 1:
                    raise StopIteration
                fw.barrier()
                arena.off = mark_persist
                gq_s = B(arena.take(NT * 128).rearrange("p (t c) -> p t c", c=128), "gq")
                gk_s = B(arena.take(NT * 128).rearrange("p (t c) -> p t c", c=128), "gk")
                gv_s = [B(arena.take(256), "gv%d" % i) for i in range(NT)]
                sp_s = [B(arena.take(256).rearrange("p (d c) -> p d c", d=2), "sp%d" % i) for i in range(NT)]
                glrT1 = B(arena.take(256, parts=17).rearrange("p (d t) -> p d t", d=2), "glrT1")
                t_e = B(arena.take(256))
                Eb = [[B(arena.take(128)) for _ in range(3)] for _ in range(2)]
                dec = [B(arena.take(2)) for _ in range(2)]
                qe = [B(arena.take(128)) for _ in range(2)]; ke = [B(arena.take(128)) for _ in range(2)]; kd = [B(arena.take(128)) for _ in range(2)]
                qlo = [B(arena.take(128)) for _ in range(2)]; qhi = [B(arena.take(128)) for _ in range(2)]; keT = [B(arena.take(128)) for _ in range(2)]
                ATs = [B(arena.take(128)) for _ in range(2)]
                Sb = [B(arena.take(256), "S%d" % i) for i in range(3)]
                for bq in qlo + qhi:
                    memset("dve", bq, bq[:], 0.0)
                memset("dve", glrT1, glrT1[:], 1.0)
                for h in range(4):
                    wa_ = nextW(); wb_ = nextW()
                    wload(wa_, win, 6 + h)
                    wload(wb_, win, 10 + h)
                    for t in range(NT):
                        make_hT(xt[t], 0, l, job, 1)
                        p1 = PS(); proj(0, wa_, 256, p1, p1[:, 0:256])
                        p2 = PS(); proj(0, wb_, 256, p2, p2[:, 0:256])
                        cp("act", gq_s, gq_s[:, t, :], p1, p1[:, 0:128])
                        cp("act", gk_s, gk_s[:, t, :], p1, p1[:, 128:256])
                        cp("dve", gv_s[t], gv_s[t][:], p2, p2[:, 0:256])
                        pz = PS()
                        for d in range(2):
                            tr(pz, pz[0:16, d * 128:(d + 1) * 128], glr_sb, glr_sb[:, t, d * 16:(d + 1) * 16])
                        cp("dve", glrT1, glrT1[0:16, :, :], pz, pz[0:16, 0:256].rearrange("p (d t) -> p d t", d=2))
                        for d in range(2):
                            mm(pz, pz[:, 256 + d * 128:256 + (d + 1) * 128], glrT1, glrT1[0:17, d, :], gw2b, gw2b[0:17, d, h * 128:(h + 1) * 128], True, True)
                        act(t_e, t_e[:], pz, pz[:, 256:512], AF.Exp, scale=-1.0)
                        act(sp_s[t], sp_s[t][:].rearrange("p d c -> p (d c)"), t_e, t_e[:], AF.Ln, bias=1.0)
                    if os.environ.get("GSTOP") == "1":
                        raise StopIteration
                    scan_jobs = [(None, d) for d in range(2)] if is_sample else [(j, d) for j in range(len(useq)) for d in range(2)]
                    for (sj, d) in scan_jobs:
                        if is_sample:
                            load("sp", Sb[0], Sb[0][:], sg_d[l, d, h])
                            order = list(range(NOWN)) if d == 0 else list(range(NT - 1, -1, -1))
                        else:
                            memset("dve", Sb[0], Sb[0][:], 0.0)
                            order = [2 * sj, 2 * sj + 1] if d == 0 else [2 * sj + 1, 2 * sj]
                        si = 0
                        for it, t in enumerate(order):
                            z = it % 2
                            c1, c2 = (0, 1) if d == 0 else (1, 0)
                            S0, S1, S2 = Sb[si % 3], Sb[(si + 1) % 3], Sb[(si + 2) % 3]
                            si += 2
                            spb = sp_s[t]; spa = sp_s[t][:, d, :]
                            pa = PS()
                            mm(pa, pa[:, 0:128], tri, tri[:, d, :], spb, spa, True, True)
                            mm(pa, pa[:, 128:256], tri, tri[:, 2 + d, :], spb, spa, True, True)
                            mm(pa, pa[:, 256:258], spb, spa, cind, cind[:], True, True)
                            E, Ei, Dd = Eb[z]
                            act(E, E[:], pa, pa[:, 0:128], AF.Exp)
                            act(Ei, Ei[:], pa, pa[:, 0:128], AF.Exp, scale=-1.0)
                            act(Dd, Dd[:], pa, pa[:, 128:256], AF.Exp)
                            act(dec[z], dec[z][:], pa, pa[:, 256:258], AF.Exp)
                            gstop(2)
                            stt(qe[z], qe[z][:], gq_s, gq_s[:, t, :], 128.0 ** -0.5, E, E[:], ALU.mult, ALU.mult)
                            tt("dve", ke[z], ke[z][:], gk_s, gk_s[:, t, :], Ei, Ei[:], ALU.mult)
                            tt("dve", kd[z], kd[z][:], gk_s, gk_s[:, t, :], Dd, Dd[:], ALU.mult)
                            gstop(21)
                            pb = PS()
                            tr(pb, pb[:, 0:128], qe[z], qe[z][:])
                            tr(pb, pb[:, 128:256], ke[z], ke[z][:])
                            gstop(22)
                            cp("act", qlo[z], qlo[z][:, 0:64], pb, pb[:, 0:64])
                            gstop(23)
                            cp("act", qhi[z], qhi[z][:, 64:128], pb, pb[:, 64:128])
                            gstop(24)
                            cp("act", keT[z], keT[z][:], pb, pb[:, 128:256])
                            gstop(3)
                            need_o = t < NOWN
                            if need_o:
                                mm(pb, pb[:, 256:384], keT[z], keT[z][:], qlo[z], qlo[z][:], True, False)
                                mm(pb, pb[:, 256:384], keT[z], keT[z][:], qhi[z], qhi[z][:], False, True)
                                tt("dve", ATs[z], ATs[z][:], pb, pb[:, 256:384], tri, tri[:, 4 + d, :], ALU.mult)
                            gstop(4)
                            pc = PS(); pc2 = PS()
                            qhalf = {0: qlo[z], 1: qhi[z]}
                            mm(pc, pc[:, 0:256], kd[z], kd[z][c1 * 64:(c1 + 1) * 64, :], gv_s[t], gv_s[t][c1 * 64:(c1 + 1) * 64, :], True, True)
                            stt(S1, S1[:], S0, S0[:], dec[z][:, c1:c1 + 1], pc, pc[:, 0:256], ALU.mult, ALU.add, extra=[dec[z]])
                            mm(pc2, pc2[:, 0:256], kd[z], kd[z][c2 * 64:(c2 + 1) * 64, :], gv_s[t], gv_s[t][c2 * 64:(c2 + 1) * 64, :], True, True)
                            stt(S2, S2[:], S1, S1[:], dec[z][:, c2:c2 + 1], pc2, pc2[:, 0:256], ALU.mult, ALU.add, extra=[dec[z]])
                            gstop(5)
                            if need_o:
                                po = PS()
                                mm(po, po[:, 0:256], ATs[z], ATs[z][:], gv_s[t], gv_s[t][:], True, False)
                                mm(po, po[:, 0:256], qhalf[c1], qhalf[c1][:], S0, S0[:], False, False)
                                mm(po, po[:, 0:256], qhalf[c2], qhalf[c2][:], S1, S1[:], False, True)
                                oo = o_all[t]
                                if d == 0:
                                    cp("act", oo, oo[:, h * 256:(h + 1) * 256], po, po[:, 0:256])
                                else:
                                    tt("dve", oo, oo[:, h * 256:(h + 1) * 256], oo, oo[:, h * 256:(h + 1) * 256], po, po[:, 0:256], ALU.add)
                            gstop(6)
                        gstop(7)
                        Sf = Sb[si % 3]
                        if not is_sample:
                            store("sp", Sf, ns_d[useq[sj], l, d, h], Sf[:])

                if dbg == 2:
                    raise StopIteration
                fw.barrier()
                arena.off = mark_persist
                gate_bcast(l, job, 16)
                q_nf = arena.take(384); q_rf = arena.take(384)
                q_n = B(q_nf.rearrange("p (h d) -> p h d", d=64), "q_n")
                q_r = B(q_rf.rearrange("p (h d) -> p h d", d=64), "q_r")
                q_sq = B(arena.take(256)); q_ss = B(arena.take(4)); q_a = B(arena.take(128)); q_b = B(arena.take(128))
                qT = B(arena.take(512), "qT")
                Pex = [B(arena.take(512)) for _ in range(2)]
                rec = B(arena.take(4))
                attn = [B(arena.take(1024), "attn%d" % i) for i in range(2)]
                ya = [B(arena.take(1024), "ya%d" % i) for i in range(2)]
                Tb = [B(arena.take(1024).rearrange("p (c t) -> p c t", c=8), "Tb%d" % i) for i in range(2)]
                gs = B(arena.take(256)); gtmp = B(arena.take(256)); g_ss = B(arena.take(1)); g_sq = B(arena.take(256))
                memset("dve", q_n, q_n[:], 0.0)
                memset("dve", q_r, q_r[:], 0.0)
                for g0 in range(NOWN // 2):
                    tiles = [2 * g0, 2 * g0 + 1]
                    for s in range(2):
                        make_hT(xt[tiles[s]], s, l, job, 1)
                    for b in range(4):
                        wq_ = nextW()
                        wload(wq_, win, b)
                        half = b // 2; jj = b % 2
                        prt = slice(half * 64, half * 64 + 64)
                        for s in range(2):
                            tg = tiles[s]
                            pq = PS(); proj(s, wq_, 256, pq, pq[:, 0:256])
                            act(q_sq, q_sq[:], pq, pq[:, 0:256], AF.Square)
                            op("dve", lambda e: e.tensor_reduce(out=q_ss[:], in_=q_sq[:].rearrange("p (h d) -> p h d", d=64), op=ALU.add, axis=AX.X), [q_sq], [q_ss])
                            rstd_from(q_ss, q_ss[:], q_ss, q_ss[:], 1.0 / 64, RMS_EPS)
                            tt("dve", q_n, q_n[:, 1:5, :], pq, pq[:, 0:256].rearrange("p (h d) -> p h d", d=64),
                               q_ss, q_ss[:].unsqueeze(2).to_broadcast([128, 4, 64]), ALU.mult)
                            tt("dve", q_n, q_n[:, 1:5, :], q_n, q_n[:, 1:5, :], qn_bc, qn_bc[:].unsqueeze(1).to_broadcast([128, 4, 64]), ALU.mult)
                            if is_sample:
                                do_rope(q_n, q_r, 4, tg, q_a, q_b, pad=1)
                                qs = q_r; qsf = q_rf
                            else:
                                qs = q_n; qsf = q_nf
                            pt = PS()
                            for r in range(4):
                                if half == 0:
                                    tr(pt, pt[:, r * 128:(r + 1) * 128], qs, qsf[:, (r + 1) * 64:(r + 3) * 64])
                                else:
                                    tr(pt, pt[:, r * 128:(r + 1) * 128], qs, qsf[:, r * 64:(r + 2) * 64])
                            cp("act", qT, qT[prt, :], pt, pt[prt, :])
                            pvb = psb[7]
                            kcs = list(range(NKC)) if is_sample else [2 * g0, 2 * g0 + 1]
                            for ik, kc in enumerate(kcs):
                                psc = PS()
                                mm(psc, psc[:, :], kT, kT[prt, jj, kc * 128:(kc + 1) * 128], qT, qT[prt, :], True, True)
                                pe_ = Pex[ik % 2]
                                act(pe_, pe_[:], psc, psc[:, :], AF.Exp, scale=0.125)
                                for r in range(4):
                                    mm(pvb, pvb[:, r * 65:(r + 1) * 65], pe_, pe_[:, r * 128:(r + 1) * 128], vA, vA[:, kc, b, :],
                                       (ik == 0 and r == 0), (ik == len(kcs) - 1 and r == 3), skip=True)
                            pv3 = pvb[:, 0:260].rearrange("p (r d) -> p r d", d=65)
                            op("dve", lambda e, o=rec[:], i=pv3[:, :, 64]: e.reciprocal(out=o, in_=i), [pvb], [rec])
                            tt("dve", attn[s], attn[s][:, b * 256:(b + 1) * 256].rearrange("p (r d) -> p r d", d=64), pvb, pv3[:, :, 0:64],
                               rec, rec[:].unsqueeze(2).to_broadcast([128, 4, 64]), ALU.mult)
                    if dbg == 4:
                        store("sp", attn[0], y_dst[0:128, :], attn[0][:])
                        store("sp", attn[1], y_dst[128:256, :], attn[1][:])
                        store("sp", o_all[0], y_dst[256:384, :], o_all[0][:])
                        store("sp", o_all[NT - 1], y_dst[384:512, :], o_all[NT - 1][:])
                        raise StopIteration
                    for b in range(4):
                        wg_ = nextW()
                        wload(wg_, win, 14 + b)
                        for s in range(2):
                            oo = o_all[tiles[s]]
                            osl = oo[:, b * 256:(b + 1) * 256]
                            pg = PS(); proj(s, wg_, 256, pg, pg[:, 0:256])
                            act(gtmp, gtmp[:], pg, pg[:, 0:256], AF.Silu)
                            act(g_sq, g_sq[:], oo, osl, AF.Square, accum=g_ss[:], accb=g_ss)
                            rstd_from(g_ss, g_ss[:], g_ss, g_ss[:], 1.0 / 256, RMS_EPS)
                            stt(oo, osl, oo, osl, g_ss[:, 0:1], glan_bc, glan_bc[:], ALU.mult, ALU.mult, extra=[g_ss])
                            tt("dve", oo, osl, oo, osl, gtmp, gtmp[:], ALU.mult)
                    for s in range(2):
                        for half in range(2):
                            p = PS()
                            for c in range(4):
                                tr(p, p[:, c * 128:(c + 1) * 128], attn[s], attn[s][:, (half * 4 + c) * 128:(half * 4 + c + 1) * 128])
                            cp("act", Tb[s], Tb[s][:, half * 4:half * 4 + 4, :], p, p[:].rearrange("p (c t) -> p c t", c=4))
                    for b in range(4):
                        w0_ = nextW(); w1_ = nextW()
                        wload(w0_, wao_d[l], b)
                        wload(w1_, win, 18 + b)
                        for s in range(2):
                            pgm = PS(); proj(s, w1_, 256, pgm, pgm[:, 0:256])
                            act(gs, gs[:], pgm, pgm[:, 0:256], AF.Sigmoid)
                            py = PS()
                            for kc in range(8):
                                mm(py, py[:, 0:256], Tb[s], Tb[s][:, kc, :], w0_, w0_[:, kc, :], kc == 0, kc == 7)
                            tt("dve", ya[s], ya[s][:, b * 256:(b + 1) * 256], py, py[:, 0:256], gs, gs[:], ALU.mult)
                    for s in range(2):
                        oo = o_all[tiles[s]]
                        for half in range(2):
                            p = PS()
                            for c in range(4):
                                tr(p, p[:, c * 128:(c + 1) * 128], oo, oo[:, (half * 4 + c) * 128:(half * 4 + c + 1) * 128])
                            cp("act", Tb[s], Tb[s][:, half * 4:half * 4 + 4, :], p, p[:].rearrange("p (c t) -> p c t", c=4))
                    for b in range(4):
                        w0_ = nextW(); w1_ = nextW()
                        wload(w0_, wgo_d[l], b)
                        wload(w1_, win, 22 + b)
                        for s in range(2):
                            pgm = PS(); proj(s, w1_, 256, pgm, pgm[:, 0:256])
                            act(gs, gs[:], pgm, pgm[:, 0:256], AF.Sigmoid)
                            py = PS()
                            for kc in range(8):
                                mm(py, py[:, 0:256], Tb[s], Tb[s][:, kc, :], w0_, w0_[:, kc, :], kc == 0, kc == 7)
                            tt("dve", gtmp, gtmp[:], py, py[:, 0:256], gs, gs[:], ALU.mult)
                            tt("dve", ya[s], ya[s][:, b * 256:(b + 1) * 256], ya[s], ya[s][:, b * 256:(b + 1) * 256], gtmp, gtmp[:], ALU.add)
                    for s in range(2):
                        for half in range(2):
                            p = PS()
                            for c in range(4):
                                tr(p, p[:, c * 128:(c + 1) * 128], ya[s], ya[s][:, (half * 4 + c) * 128:(half * 4 + c + 1) * 128])
                            cp("act", Tb[s], Tb[s][:, half * 4:half * 4 + 4, :], p, p[:].rearrange("p (c t) -> p c t", c=4))
                    for b in range(4):
                        wo_ = nextW()
                        wload(wo_, wout_d[l], b)
                        for s in range(2):
                            py = PS()
                            for kc in range(8):
                                mm(py, py[:, 0:256], Tb[s], Tb[s][:, kc, :], wo_, wo_[:, kc, :], kc == 0, kc == 7)
                            tt("dve", attn[s], attn[s][:, b * 256:(b + 1) * 256], py, py[:, 0:256], gate_bc, gate_bc[:, b * 256:(b + 1) * 256], ALU.mult)
                    for s in range(2):
                        xb = xt[tiles[s]]
                        stt(attn[s], attn[s][:], xb, xb[:], ALPHA, attn[s], attn[s][:], ALU.mult, ALU.add)
                        layer_norm(attn[s], attn[s][:], xb, xb[:], small)

                if dbg == 3:
                    for t in range(NT):
                        store("sp", xt[t], yrow(t), xt[t][:])
                    raise StopIteration
                fw.barrier()
                arena.reset()
                small = {"st6": B(arena.take(12)), "mv": B(arena.take(2)), "rs": B(arena.take(1))}
                load("sp", lng, lng[:], ln2g_d[l].partition_broadcast(128))
                load("sp", lnb, lnb[:], ln2b_d[l].partition_broadcast(128))
                gate_bcast(l, job, 40)
                if do_peer:
                    peer_phase(l, job, NOWN, small)
                else:
                    for t in range(NOWN):
                        pre = B(arena.take(1024)) if t == 0 else pre
                        op("dve", lambda e, o=pre[:], i=xt[t][:]: e.tensor_scalar(out=o, in0=i, scalar1=ALPHA, scalar2=None, op0=ALU.mult, op1=ALU.bypass), [xt[t]], [pre])
                        layer_norm(pre, pre[:], xt[t], xt[t][:], small)
                if l == depth - 1:
                    for t in range(NOWN):
                        store("sp", xt[t], yrow(t), xt[t][:])

        def do_rope(src, dst, H, tile_idx, ta, tb, pad=0):
            if pad:
                s5 = src[:, pad:pad + H, :].rearrange("p h (a f r) -> p h a f r", a=2, f=2)
                d5 = dst[:, pad:pad + H, :].rearrange("p h (a f r) -> p h a f r", a=2, f=2)
            else:
                s5 = src[:].rearrange("p (h a f r) -> p h a f r", h=H, a=2, f=2)
                d5 = dst[:].rearrange("p (h a f r) -> p h a f r", h=H, a=2, f=2)
            cos = rope[:, tile_idx, 0, :].rearrange("p (a r) -> p a r", a=2).unsqueeze(1).to_broadcast([128, H, 2, 16])
            sin = rope[:, tile_idx, 1, :].rearrange("p (a r) -> p a r", a=2).unsqueeze(1).to_broadcast([128, H, 2, 16])
            x1 = s5[:, :, :, 0, :]; x2 = s5[:, :, :, 1, :]
            a4 = ta[:].rearrange("p (h a r) -> p h a r", h=H, a=2)
            b4 = tb[:].rearrange("p (h a r) -> p h a r", h=H, a=2)
            tt("dve", ta, a4, src, x1, rope, cos, ALU.mult)
            tt("dve", tb, b4, src, x2, rope, sin, ALU.mult)
            tt("dve", dst, d5[:, :, :, 0, :], ta, a4, tb, b4, ALU.subtract)
            tt("dve", ta, a4, src, x2, rope, cos, ALU.mult)
            tt("dve", tb, b4, src, x1, rope, sin, ALU.mult)
            tt("dve", dst, d5[:, :, :, 1, :], ta, a4, tb, b4, ALU.add)

        PV_ = os.environ.get("PEERV", "")
        PEMOD = int(os.environ.get("PEMOD", "1000"))
        RATE1 = int(os.environ.get("RATE1", "2")); RATE2 = int(os.environ.get("RATE2", "4"))

        def peer_phase(l, job, NT, small):
            t_sk = B(arena.take(128))
            for p_ in range(2):
                load("sp", t_sk, t_sk[:], sk_d[l, p_])
                pp = PS()
                tr(pp, pp[:, 0:128], t_sk, t_sk[:])
                cp("act", skT, skT[:, p_, :], pp, pp[:, 0:128])
            qpT = B(arena.take(16 * 256).rearrange("p (q s t) -> p q s t", q=16, s=2), "qpT")
            s_sb = B(arena.take(2048).rearrange("p (q n) -> p q n", q=16), "s_sb")
            s_wk = B(arena.take(2048).rearrange("p (q n) -> p q n", q=16), "s_wk")
            tv = B(arena.take(256).rearrange("p (q k) -> p q k", q=16), "tv")
            ti = B(arena.take(256).rearrange("p (q k) -> p q k", q=16), "ti")
            ti_t = arena_t[:, arena.off - 256:arena.off].bitcast(U32).rearrange("p (q k) -> p q k", q=16)
            tif = B(arena.take(256).rearrange("p (q k) -> p q k", q=16), "tif")
            cand = Buf(s_wk[:].rearrange("p q n -> p (q n)").rearrange("p (h c) -> p h c", h=8), "cand", base=s_wk)
            cwk = Buf(s_sb[:].rearrange("p q n -> p (q n)").rearrange("p (h c) -> p h c", h=8), "cwk", base=s_sb)
            sc = B(arena.take(128).rearrange("p (h k) -> p h k", h=8), "sc")
            ci = B(arena.take(128), "ci")
            ci_u = arena_t[:, arena.off - 128:arena.off].bitcast(U32).rearrange("p (h k) -> p h k", h=8)
            ca = B(arena.take(128), "ca"); ca_u = arena_t[:, arena.off - 128:arena.off].bitcast(U32)
            cb = B(arena.take(128), "cb"); cb_u = arena_t[:, arena.off - 128:arena.off].bitcast(U32)
            caf = B(arena.take(128).rearrange("p (h k) -> p h k", h=8), "caf")
            cbf = B(arena.take(128).rearrange("p (h k) -> p h k", h=8), "cbf")
            oh = cwk
            oh4 = cwk[:].rearrange("p h (k a) -> p h k a", k=16)
            i1s = B(arena.take(128).rearrange("p (h k) -> p h k", h=8), "i1s")
            i2s = B(arena.take(128).rearrange("p (h k) -> p h k", h=8), "i2s")
            idxf = B(arena.take(128), "idxf")
            idxu2, idxu2_u, gw2, h2b2 = [], [], [], []
            for i in range(2):
                idxu2.append(B(arena.take(128), "idxu%d" % i)); idxu2_u.append(arena_t[:, arena.off - 128:arena.off].bitcast(U32))
                gw2.append(B(arena.take(128).rearrange("p (h k) -> p h k", h=8), "gw%d" % i))
                h2b2.append(B(arena.take(1024), "h2b%d" % i))
            gsum = B(arena.take(8), "gsum")
            av = B(arena.take(128), "av")
            wv = B(arena.take(128), "wv")
            NB = int(os.environ.get("NBUF", "6"))
            GB = [B(arena.take(2048), "G%d" % i) for i in range(NB)]
            gctr = [0]
            NR = 8
            av_r = [B(arena.take(1), "av%d" % i) for i in range(NR)]
            wt_r = [B(arena.take(1), "wt%d" % i) for i in range(NR)]
            wv_r = [B(arena.take(1), "wv%d" % i) for i in range(NR)]
            DG = [B(arena.take(128), "DG%d" % i) for i in range(4)]
            dgc = [0]
            oth = B(arena.take(1024), "oth")
            junk = oth
            if os.environ.get("ARENA_DBG"):
                print("PEER arena used", arena.off, "of", arena.size)
            nseg = 16
            NG = NT // 2

            def stageA(g0):
                tiles = [2 * g0, 2 * g0 + 1]
                for s in range(2):
                    make_hT(xt[tiles[s]], s, l, job, 2)
                    yield
                for blk in range(8):
                    w = nextW()
                    wload(w, wq_d[l], blk)
                    for qq in range(2):
                        hp = blk * 2 + qq
                        pq = PS()
                        for s in range(2):
                            for kc in range(8):
                                mm(pq, pq[:, s * 128:(s + 1) * 128], w, w[:, kc, qq * 128:(qq + 1) * 128], hT[s], hT[s][:, kc, :], kc == 0, kc == 7)
                            yield
                        cp("act", qpT, qpT[:, hp, :, :], pq, pq[:, 0:256].rearrange("p (s t) -> p s t", s=2))
                        yield

            def stageB(g0, s):
                idxu = idxu2[s]; idxu_u = idxu2_u[s]; gw = gw2[s]; h2b = h2b2[s]
                for bq in range(4):
                    pscr = PS()
                    for q4 in range(4):
                        hp = bq * 4 + q4
                        mm(pscr, pscr[:, q4 * 128:(q4 + 1) * 128], qpT, qpT[:, hp, s, :], skT, skT[:, hp % 2, :], True, True)
                    cp("act", s_sb, s_sb[:, bq * 4:bq * 4 + 4, :], pscr, pscr[:].rearrange("p (q n) -> p q n", q=4))
                    yield
                for half in range(2):
                    p = PS()
                    for c in range(4):
                        tr(p, p[:, c * 128:(c + 1) * 128], hT[s], hT[s][:, half * 4 + c, :])
                    cp("act", h2b, h2b[:, half * 512:(half + 1) * 512], p, p[:])
                    yield
                for q in range(nseg):
                    op("dve", lambda e, o=tv[:, q, 0:8], i=s_sb[:, q, :]: e.max(out=o, in_=i), [s_sb], [tv])
                    yield
                    op("dve", lambda e, o=ti_t[:, q, 0:8], m=tv[:, q, 0:8], i=s_sb[:, q, :]: e.max_index(out=o, in_max=m, in_values=i), [s_sb, tv], [ti])
                    yield
                    op("dve", lambda e, o=s_wk[:, q, :], m=tv[:, q, 0:8], i=s_sb[:, q, :]: e.match_replace(out=o, in_to_replace=m, in_values=i, imm_value=NEG), [s_sb, tv], [s_wk])
                    yield
                    op("dve", lambda e, o=tv[:, q, 8:16], i=s_wk[:, q, :]: e.max(out=o, in_=i), [s_wk], [tv])
                    yield
                    op("dve", lambda e, o=ti_t[:, q, 8:16], m=tv[:, q, 8:16], i=s_wk[:, q, :]: e.max_index(out=o, in_max=m, in_values=i), [s_wk, tv], [ti])
                    yield
                cp("dve", tif, tif[:], ti, ti_t)
                tv4 = tv[:].rearrange("p (h q) k -> p h q k", q=2)
                tif4 = tif[:].rearrange("p (h q) k -> p h q k", q=2)
                c4 = cand[:].rearrange("p h (a b) -> p h a b", a=16)
                for h in range(8):
                    tt("dve", cand, c4[:, h, :, :], tv, tv4[:, h, 0, :].unsqueeze(2).to_broadcast([128, 16, 16]),
                       tv, tv4[:, h, 1, :].unsqueeze(1).to_broadcast([128, 16, 16]), ALU.add)
                    yield
                for h in range(8):
                    op("dve", lambda e, o=sc[:, h, 0:8], i=cand[:, h, :]: e.max(out=o, in_=i), [cand], [sc])
                    yield
                    op("dve", lambda e, o=ci_u[:, h, 0:8], m=sc[:, h, 0:8], i=cand[:, h, :]: e.max_index(out=o, in_max=m, in_values=i), [cand, sc], [ci])
                    yield
                    op("dve", lambda e, o=cwk[:, h, :], m=sc[:, h, 0:8], i=cand[:, h, :]: e.match_replace(out=o, in_to_replace=m, in_values=i, imm_value=NEG), [cand, sc], [cwk])
                    yield
                    op("dve", lambda e, o=sc[:, h, 8:16], i=cwk[:, h, :]: e.max(out=o, in_=i), [cwk], [sc])
                    yield
                    op("dve", lambda e, o=ci_u[:, h, 8:16], m=sc[:, h, 8:16], i=cwk[:, h, :]: e.max_index(out=o, in_max=m, in_values=i), [cwk, sc], [ci])
                    yield
                op("dve", lambda e: e.tensor_single_scalar(out=ca_u, in_=ci_u.rearrange("p h k -> p (h k)"), scalar=4, op=ALU.logical_shift_right), [ci], [ca])
                op("dve", lambda e: e.tensor_single_scalar(out=cb_u, in_=ci_u.rearrange("p h k -> p (h k)"), scalar=15, op=ALU.bitwise_and), [ci], [cb])
                yield
                cp("dve", caf, caf[:].rearrange("p h k -> p (h k)"), ca, ca_u)
                cp("dve", cbf, cbf[:].rearrange("p h k -> p (h k)"), cb, cb_u)
                yield
                io_b = iota[:].unsqueeze(1).to_broadcast([128, 16, 16])
                for (cf, col, dst) in ((caf, 0, i1s), (cbf, 1, i2s)):
                    for h in range(8):
                        tt("dve", oh, oh4[:, h, :, :], cf, cf[:, h, :].unsqueeze(2).to_broadcast([128, 16, 16]), iota, io_b, ALU.is_equal)
                        yield
                        tt("dve", oh, oh4[:, h, :, :], oh, oh4[:, h, :, :], tif, tif4[:, h, col, :].unsqueeze(1).to_broadcast([128, 16, 16]), ALU.mult)
                        yield
                    op("dve", lambda e, o=dst[:].rearrange("p h k -> p (h k)"), i=oh4.rearrange("p h k a -> p (h k) a"): e.tensor_reduce(out=o, in_=i, op=ALU.add, axis=AX.X), [oh], [dst])
                    yield
                stt(idxf, idxf[:], i1s, i1s[:].rearrange("p h k -> p (h k)"), 128.0, i2s, i2s[:].rearrange("p h k -> p (h k)"), ALU.mult, ALU.add)
                cp("dve", idxu, idxu_u, idxf, idxf[:])
                yield
                tt("dve", gw, gw[:], sc, sc[:], sc, sc[:, :, 0:1].to_broadcast([128, 8, 16]), ALU.subtract)
                act(gw, gw[:], gw, gw[:], AF.Exp)
                yield
                op("dve", lambda e: e.tensor_reduce(out=gsum[:], in_=gw[:], op=ALU.add, axis=AX.X), [gw], [gsum])
                op("dve", lambda e: e.reciprocal(out=gsum[:], in_=gsum[:]), [gsum], [gsum])
                yield
                tt("dve", gw, gw[:], gw, gw[:], gsum, gsum[:].unsqueeze(2).to_broadcast([128, 8, 16]), ALU.mult)
                yield

            def advance(bg, n):
                for _ in range(n):
                    while bg:
                        try:
                            next(bg[0])
                            break
                        except StopIteration:
                            bg.pop(0)
                    if not bg:
                        return False
                return True

            def drain(bg):
                while bg:
                    for _ in bg[0]:
                        pass
                    bg.pop(0)

            def stageC(g0, s, bg, rate):
                xb = xt[2 * g0 + s]
                idxu = idxu2[s]; idxu_u = idxu2_u[s]; gw = gw2[s]; h2b = h2b2[s]
                gwf = gw[:].rearrange("p h k -> p (h k)")
                pacc = [psb[6], psb[7]]
                dve_hks = [hk for hk in range(128) if hk % PEMOD == 0] if PEMOD < 128 else []
                pe_hks = [hk for hk in range(128) if hk not in dve_hks]
                if dve_hks:
                    memset("dve", oth, oth[:], 0.0)
                for hk in range(128):
                    gb = GB[gctr[0] % NB]; gctr[0] += 1
                    r = hk % NR
                    fw.swdma(lambda e, o=gb[:], i=idxu_u[:, hk:hk + 1]: e.indirect_dma_start(out=o, out_offset=None, in_=puv_d[l], in_offset=bass.IndirectOffsetOnAxis(ap=i, axis=0)),
                             gb, extra_reads=[idxu])
                    stt(gb, gb[:, 0:1024], h2b, h2b[:], 1.0, gb, gb[:, 0:1024], ALU.mult, ALU.mult, accum=av_r[r][:, 0:1], accb=av_r[r])
                    act(wt_r[r], wt_r[r][:, 0:1], av_r[r], av_r[r][:, 0:1], AF.Gelu_apprx_tanh)
                    advance(bg, rate)
                    tt("dve", wv_r[r], wv_r[r][:, 0:1], wt_r[r], wt_r[r][:, 0:1], gw, gwf[:, hk:hk + 1], ALU.mult)
                    if hk in pe_hks:
                        dg = DG[dgc[0] % 4]; dgc[0] += 1
                        act(dg, dg[:], ident, ident[:], AF.Copy, scale=wv_r[r][:, 0:1], extra=[wv_r[r]])
                        for hf in range(2):
                            mm(pacc[hf], pacc[hf][:, :], dg, dg[:], gb, gb[:, 1024 + hf * 512:1024 + (hf + 1) * 512], hk == pe_hks[0], hk == pe_hks[-1])
                    else:
                        stt(oth, oth[:], gb, gb[:, 1024:2048], wv_r[r][:, 0:1], oth, oth[:], ALU.mult, ALU.add, extra=[wv_r[r]])
                for hf in range(2):
                    if dve_hks:
                        tt("dve", oth, oth[:, hf * 512:(hf + 1) * 512], oth, oth[:, hf * 512:(hf + 1) * 512], pacc[hf], pacc[hf][:, :], ALU.add)
                        tt("dve", oth, oth[:, hf * 512:(hf + 1) * 512], oth, oth[:, hf * 512:(hf + 1) * 512], gate_bc, gate_bc[:, hf * 512:(hf + 1) * 512], ALU.mult)
                    else:
                        tt("dve", oth, oth[:, hf * 512:(hf + 1) * 512], pacc[hf], pacc[hf][:, :], gate_bc, gate_bc[:, hf * 512:(hf + 1) * 512], ALU.mult)
                stt(oth, oth[:], xb, xb[:], ALPHA, oth, oth[:], ALU.mult, ALU.add)
                layer_norm(oth, oth[:], xb, xb[:], small)

            drain([stageA(0), stageB(0, 0)])
            for g0 in range(NG):
                for s in range(2):
                    if s == 0:
                        bg = [stageB(g0, 1)]; rate = RATE1
                    elif g0 + 1 < NG:
                        bg = [stageA(g0 + 1), stageB(g0 + 1, 0)]; rate = RATE2
                    else:
                        bg = []; rate = 0
                    stageC(g0, s, bg, rate)
                    drain(bg)

        try:
            for u in range(0, n_ctx, 2):
                run_unit(False, list(range(u, min(u + 2, n_ctx))))
            if do_sample:
                run_unit(True, 0)
        except StopIteration:
            pass
        fw.barrier()
        fw.run_block()
    return nc


def host_consts():
    ident = np.eye(128, dtype=np.float32)
    j = np.arange(128)[:, None]; i = np.arange(128)[None, :]
    same = (j // 64) == (i // 64)
    s = -1.0 / 16.0
    tri = np.zeros((128, 6, 128), np.float32)
    tri[:, 0, :] = np.where(same & (j <= i), s, 0.0)
    tri[:, 1, :] = np.where(same & (j >= i), s, 0.0)
    tri[:, 2, :] = np.where(same & (j > i), s, 0.0)
    tri[:, 3, :] = np.where(same & (j < i), s, 0.0)
    tri[:, 4, :] = np.where(same & (i >= j), 1.0, 0.0)
    tri[:, 5, :] = np.where(same & (i <= j), 1.0, 0.0)
    cind = np.zeros((128, 2), np.float32)
    cind[0:64, 0] = s; cind[64:128, 1] = s
    pos = np.arange(1024)
    r = (pos // 64).astype(np.float32); col = (pos % 64).astype(np.float32)
    inv = (np.float32(10000.0) ** (-np.arange(16, dtype=np.float32) / np.float32(16))).astype(np.float32)
    ang = np.concatenate([r[:, None] * inv[None, :], col[:, None] * inv[None, :]], axis=1).astype(np.float32)
    rope = np.zeros((128, 8, 2, 32), np.float32)
    angt = ang.reshape(8, 128, 32).transpose(1, 0, 2)
    rope[:, :, 0, :] = np.cos(angt); rope[:, :, 1, :] = np.sin(angt)
    iota = np.tile(np.arange(16, dtype=np.float32)[None, :], (128, 1))
    return {"c_ident": ident, "c_tri": tri, "c_cind": cind, "c_rope": rope, "c_iota": iota}


def _pack_blocks(w, col_lists):
    L = w.shape[0]
    out = np.empty((L, len(col_lists), 128, 2048), np.float32)
    for j, ranges in enumerate(col_lists):
        blk = np.concatenate([w[:, :, a:b] for (a, b) in ranges], axis=2)
        out[:, j] = blk.reshape(L, 8, 128, 256).transpose(0, 2, 1, 3).reshape(L, 128, 2048)
    return out


def make_in_maps(inp, n_cores=8):
    f = lambda a: np.ascontiguousarray(np.asarray(a, dtype=np.float32))
    consts = host_consts()
    ada_b = f(inp["ada_b"])
    ada_bT = np.ascontiguousarray(ada_b.reshape(2, 48, 128).transpose(2, 0, 1))
    gw2 = f(inp["gate_w2"]); gb = f(inp["gate_b"])
    gw2b = np.zeros((2, 17, 2, 512), np.float32)
    gw2b[:, 0:16, :, :] = gw2.transpose(0, 2, 1, 3)
    gw2b[:, 16, :, :] = gb
    shared = {
        "ada_w_t": _pack_blocks(f(inp["ada_w"]), [[(b * 256, (b + 1) * 256)] for b in range(24)]), "ada_bT": ada_bT, "q_norm": f(inp["q_norm"]), "k_norm": f(inp["k_norm"]),
        "gw2b_c": gw2b, "gla_norm": f(inp["gla_norm"]), "w_attn_o_t": _pack_blocks(f(inp["w_attn_o"]), [[(b * 256, (b + 1) * 256)] for b in range(4)]),
        "w_gla_o_t": _pack_blocks(f(inp["w_gla_o"]), [[(b * 256, (b + 1) * 256)] for b in range(4)]),
        "w_out_t": _pack_blocks(f(inp["w_out"]), [[(b * 256, (b + 1) * 256)] for b in range(4)]), "ln1_g": f(inp["ln1_g"]), "ln1_b": f(inp["ln1_b"]), "ln2_g": f(inp["ln2_g"]), "ln2_b": f(inp["ln2_b"]),
        "peer_wq_t": _pack_blocks(f(inp["peer_wq"]), [[(b * 256, (b + 1) * 256)] for b in range(8)]), "peer_sub_keys": f(inp["peer_sub_keys"]),
        "puv0": np.ascontiguousarray(np.concatenate([f(inp["peer_u"][0]), f(inp["peer_v"][0])], axis=1)),
        "puv1": np.ascontiguousarray(np.concatenate([f(inp["peer_u"][1]), f(inp["peer_v"][1])], axis=1)),
    }
    shared.update(consts)
    w_in_full = f(inp["w_in"])
    blocks = [[(b * 256, (b + 1) * 256)] for b in range(4)]
    blocks += [[(1024, 1280)], [(1280, 1536)]]
    blocks += [[(1536 + h * 128, 1536 + (h + 1) * 128), (2048 + h * 128, 2048 + (h + 1) * 128)] for h in range(4)]
    blocks += [[(2560 + h * 256, 2560 + (h + 1) * 256)] for h in range(4)]
    blocks += [[(3584 + b * 256, 3584 + (b + 1) * 256)] for b in range(4)]
    blocks += [[(4640 + b * 256, 4640 + (b + 1) * 256)] for b in range(4)]
    blocks += [[(5664 + b * 256, 5664 + (b + 1) * 256)] for b in range(4)]
    shared["w_in_t"] = _pack_blocks(w_in_full, blocks)
    wglr = np.ascontiguousarray(w_in_full[:, :, 4608:4640])
    shared["wglr_c"] = wglr
    gw2b_rev = np.ascontiguousarray(gw2b[:, :, ::-1, :])
    wglr_rev = np.ascontiguousarray(np.concatenate([wglr[:, :, 16:32], wglr[:, :, 0:16]], axis=2))
    rope_n = consts["c_rope"]
    rope_rev = np.ascontiguousarray(rope_n[::-1, ::-1, :, :])
    xp = f(inp["x_prompt"]); xs = f(inp["x_sample"])
    ck = f(inp["cache_k"]).reshape(4, 2, 512, 256); cv = f(inp["cache_v"]).reshape(4, 2, 512, 256)
    sg = f(inp["state_gla"]); c = f(inp["c"]); cctx = f(inp["c_ctx"])
    maps = []
    for core in range(n_cores):
        b = core % 4
        cond = np.stack([cctx, c[b]], axis=1)
        condT = np.ascontiguousarray(cond.reshape(8, 128, 2).transpose(1, 0, 2))
        m = dict(shared)
        rev = core >= 4
        m.update({"xc": np.ascontiguousarray(xp[core * 4:(core + 1) * 4]),
                  "xs": np.ascontiguousarray(xs[b][::-1] if rev else xs[b]),
                  "ck": np.ascontiguousarray(ck[b]), "cv": np.ascontiguousarray(cv[b]),
                  "sg": np.ascontiguousarray(sg[b][:, ::-1] if rev else sg[b]),
                  "gw2b_s": gw2b_rev if rev else gw2b, "wglr_s": wglr_rev if rev else wglr,
                  "c_rope": rope_rev if rev else rope_n,
                  "condT": condT})
        maps.append(m)
    return maps


_NC_CACHE = {}


def kernel(**inputs):
    if "nc" not in _NC_CACHE:
        _NC_CACHE["nc"] = build_program()
    nc = _NC_CACHE["nc"]
    maps = make_in_maps(inputs)
    res = run_bass_kernel_spmd(nc, maps, core_ids=list(range(8)))
    R = res.results
    y_prompt = np.concatenate([R[i]["yc"] for i in range(8)], axis=0).astype(np.float32)
    y_sample = np.stack([np.concatenate([R[i]["ys"], R[i + 4]["ys"][::-1]], axis=0) for i in range(4)], axis=0).astype(np.float32)
    nk = np.concatenate([R[i]["nk"] for i in range(8)], axis=0).reshape(32, 2, 256, 4, 64).astype(np.float32)
    nv = np.concatenate([R[i]["nv"] for i in range(8)], axis=0).reshape(32, 2, 256, 4, 64).astype(np.float32)
    ns = np.concatenate([R[i]["ns"] for i in range(8)], axis=0).astype(np.float32)
    return (y_prompt, y_sample, nk, nv, ns)
```

```python
import os
import numpy as np
from contextlib import ExitStack
import concourse.bass as bass
import concourse.mybir as mybir
from concourse.bass_utils import run_bass_kernel_spmd

F32 = mybir.dt.float32
U32 = mybir.dt.uint32
I32 = mybir.dt.int32
AF = mybir.ActivationFunctionType
ALU = mybir.AluOpType
AX = mybir.AxisListType

DEPTH = 2
ALPHA = (2.0 * DEPTH) ** 0.25
LN_EPS = 1e-5
RMS_EPS = 1e-6
NEG = -3.0e38


class Buf:
    __slots__ = ("ap", "name", "lw", "rd", "sem", "cnt", "psum", "base")

    def __init__(self, ap, name="", psum=False, base=None):
        self.ap = ap
        self.name = name
        self.psum = psum
        self.base = base
        self.lw = None
        self.rd = []
        self.sem = None
        self.cnt = 0

    def __getitem__(self, k):
        return self.ap[k]


class SwUse:
    __slots__ = ("sem", "uid")

    def __init__(self, sem, uid):
        self.sem = sem
        self.uid = uid


class FW:
    ENG = ("pe", "act", "dve", "pool", "sp")

    def __init__(self, nc, stack):
        self.nc = nc
        self.stack = stack
        self.eng = {"pe": nc.tensor, "act": nc.scalar, "dve": nc.vector, "pool": nc.gpsimd, "sp": nc.sync}
        self.sem = {e: stack.enter_context(nc.semaphore("s_" + e)) for e in self.ENG}
        self.cnt = {e: 0 for e in self.ENG}
        self.waited = {e: {} for e in self.ENG}
        self.ops = {e: [] for e in self.ENG}
        self.nsem = 0
        self.dmabufs = []
        self.free_sems = []
        self.sw_sems = {}
        self.sw_uses = 0
        self.all_sems = []

    def sb(self, name, shape, dt=F32):
        return self.stack.enter_context(self.nc.sbuf_tensor(name, list(shape), dt))

    def ps(self, name, shape, dt=F32):
        return self.stack.enter_context(self.nc.psum_tensor(name, list(shape), dt))

    def _dsem(self, b):
        if b.sem is None:
            if self.free_sems:
                b.sem = self.free_sems.pop()
            else:
                h = self.stack.enter_context(self.nc.semaphore("d%d" % self.nsem))
                self.nsem += 1
                b.sem = [h, 0]
                self.all_sems.append(b.sem)
            self.dmabufs.append(b)
        return b.sem

    def _deps(self, e, reads, writes):
        deps = []
        for r in reads:
            if r.lw is not None:
                deps.append(r.lw)
            if r.psum:
                deps.extend(d for d in r.rd if d[0] != e)
        for w in writes:
            if w.lw is not None:
                deps.append(w.lw)
            deps.extend(w.rd)
        need = {}
        for d in deps:
            key = d[0]
            if isinstance(key, str) and key == e and e == "pe":
                continue
            if need.get(key, 0) < d[1]:
                need[key] = d[1]
        out = []
        wd = self.waited[e]
        for key, v in need.items():
            if wd.get(key, 0) >= v:
                continue
            wd[key] = v
            out.append((key, v))
        return out

    def _emit_waits(self, e, waits):
        eng = self.eng[e]
        for key, v in waits:
            s = self.sem[key] if isinstance(key, str) else (key.sem if isinstance(key, SwUse) else key)
            self.ops[e].append((lambda eng=eng, s=s, v=v: eng.wait_ge(s, v)))

    @staticmethod
    def _norm(bufs):
        out = []
        for b in bufs:
            while b.base is not None:
                b = b.base
            out.append(b)
        return out

    def op(self, e, fn, reads=(), writes=()):
        reads = self._norm(reads); writes = self._norm(writes)
        waits = self._deps(e, reads, writes)
        self._emit_waits(e, waits)
        self.cnt[e] += 1
        c = self.cnt[e]
        s = self.sem[e]
        eng = self.eng[e]
        self.ops[e].append((lambda eng=eng, fn=fn, s=s: fn(eng).then_inc(s, 1)))
        dep = (e, c)
        for r in reads:
            r.rd.append(dep)
        for w in writes:
            w.lw = dep
            w.rd = []
        return dep

    def swdma(self, fn, sb_buf, extra_reads=()):
        q = "pool"
        reads = self._norm(list(extra_reads))
        writes = [sb_buf]
        waits = self._deps(q, reads, writes)
        self._emit_waits(q, waits)
        if sb_buf.name not in self.sw_sems:
            self.sw_sems[sb_buf.name] = [self.stack.enter_context(self.nc.semaphore("g%d" % len(self.sw_sems))), 0]
        sp_ = self.sw_sems[sb_buf.name]
        sp_[1] += 16
        s, v = sp_[0], sp_[1]
        eng = self.eng[q]
        self.ops[q].append((lambda eng=eng, fn=fn, s=s: fn(eng).then_inc(s, 16)))
        dep = (s, v)
        for r in reads:
            r.rd.append(dep)
        sb_buf.lw = dep
        sb_buf.rd = []
        return dep

    def dma(self, q, fn, sb_buf, is_load, extra_reads=(), extra_writes=()):
        reads = list(extra_reads) + ([] if is_load else [sb_buf])
        writes = list(extra_writes) + ([sb_buf] if is_load else [])
        waits = self._deps(q, reads, writes)
        self._emit_waits(q, waits)
        sp_ = self._dsem(sb_buf)
        sp_[1] += 16
        v = sp_[1]
        s = sp_[0]
        eng = self.eng[q]
        self.ops[q].append((lambda eng=eng, fn=fn, s=s: fn(eng).then_inc(s, 16)))
        dep = (s, v)
        for r in reads:
            r.rd.append(dep)
        for w in writes:
            w.lw = dep
            w.rd = []
        return dep

    def barrier(self):
        waits = []
        wd = self.waited["sp"]
        for e in ("pe", "act", "dve", "pool"):
            if self.cnt[e] > wd.get(e, 0):
                wd[e] = self.cnt[e]
                waits.append((e, self.cnt[e]))
        for sp_ in self.all_sems:
            if sp_[1] > wd.get(sp_[0], 0):
                wd[sp_[0]] = sp_[1]
                waits.append((sp_[0], sp_[1]))
        for b in self.dmabufs:
            self.free_sems.append(b.sem)
            b.sem = None
        self.dmabufs = []
        self._emit_waits("sp", waits)
        self.cnt["sp"] += 1
        c = self.cnt["sp"]
        s = self.sem["sp"]
        eng = self.eng["sp"]
        self.ops["sp"].append((lambda eng=eng, s=s: eng.nop().then_inc(s, 1)))
        for e in ("pe", "act", "dve", "pool"):
            self.waited[e]["sp"] = c
            self._emit_waits(e, [("sp", c)])

    def run_block(self):
        nc = self.nc
        with nc.Block() as block:
            @block.sync
            def _(e):
                for f in self.ops["sp"]:
                    f()

            @block.tensor
            def _(e):
                for f in self.ops["pe"]:
                    f()

            @block.scalar
            def _(e):
                for f in self.ops["act"]:
                    f()

            @block.vector
            def _(e):
                for f in self.ops["dve"]:
                    f()

            @block.gpsimd
            def _(e):
                for f in self.ops["pool"]:
                    f()


class Arena:
    def __init__(self, t, size):
        self.t = t
        self.size = size
        self.off = 0

    def reset(self):
        self.off = 0

    def take(self, n, parts=128):
        assert self.off + n <= self.size, ("arena overflow", self.off, n, self.size)
        ap = self.t[0:parts, self.off:self.off + n]
        self.off += n
        return ap


def build_program(n_ctx=4, do_sample=True, depth=DEPTH, do_peer=True, dbg=None):
    nc = bass.Bass("TRN2", target_bir_lowering=False)

    def DI(name, shape, dt=F32):
        return nc.dram_tensor(name, list(shape), dt, kind="ExternalInput").ap()

    def DO(name, shape, dt=F32):
        return nc.dram_tensor(name, list(shape), dt, kind="ExternalOutput").ap()

    xc_d = DI("xc", [4, 256, 1024]); xs_d = DI("xs", [1024, 1024])
    ck_d = DI("ck", [2, 512, 256]); cv_d = DI("cv", [2, 512, 256])
    sg_d = DI("sg", [2, 2, 4, 128, 256])
    condT_d = DI("condT", [128, 8, 2])
    ada_w_d = DI("ada_w", [2, 1024, 6144]); ada_bT_d = DI("ada_bT", [128, 2, 48])
    w_in_d = DI("w_in", [2, 1024, 6688])
    qn_d = DI("q_norm", [2, 64]); kn_d = DI("k_norm", [2, 64])
    gw2b_c_d = DI("gw2b_c", [2, 17, 2, 512]); gw2b_s_d = DI("gw2b_s", [2, 17, 2, 512])
    wglr_c_d = DI("wglr_c", [2, 1024, 32]); wglr_s_d = DI("wglr_s", [2, 1024, 32])
    glan_d = DI("gla_norm", [2, 256])
    wao_d = DI("w_attn_o", [2, 1024, 1024]); wgo_d = DI("w_gla_o", [2, 1024, 1024]); wout_d = DI("w_out", [2, 1024, 1024])
    ln1g_d = DI("ln1_g", [2, 1024]); ln1b_d = DI("ln1_b", [2, 1024])
    ln2g_d = DI("ln2_g", [2, 1024]); ln2b_d = DI("ln2_b", [2, 1024])
    wq_d = DI("peer_wq", [2, 1024, 2048]); sk_d = DI("peer_sub_keys", [2, 2, 128, 128])
    puv_d = [DI("puv0", [16384, 2048]), DI("puv1", [16384, 2048])]
    ident_d = DI("c_ident", [128, 128])
    tri_d = DI("c_tri", [128, 6, 128])
    cind_d = DI("c_cind", [128, 2])
    rope_d = DI("c_rope", [128, 8, 2, 32])
    iota_d = DI("c_iota", [128, 16])

    yc_d = DO("yc", [4, 256, 1024]); ys_d = DO("ys", [512, 1024])
    nk_d = DO("nk", [4, 2, 256, 256]); nv_d = DO("nv", [4, 2, 256, 256])
    ns_d = DO("ns", [4, 2, 2, 4, 128, 256])

    with ExitStack() as st:
        fw = FW(nc, st)
        op = fw.op

        def B(ap, name=""):
            return Buf(ap, name)

        ident = B(fw.sb("ident", [128, 128]))
        tri = B(fw.sb("tri", [128, 6, 128]))
        cind = B(fw.sb("cind", [128, 2]))
        rope = B(fw.sb("rope", [128, 8, 2, 32]))
        iota = B(fw.sb("iota", [128, 16]))
        modT = B(fw.sb("modT", [128, 2, 2, 48]))
        adab = B(fw.sb("adab", [128, 2, 48]))
        scond = B(fw.sb("scond", [128, 8, 2]))
        gate_bc = B(fw.sb("gate_bc", [128, 1024]))
        lng = B(fw.sb("lng", [128, 1024])); lnb = B(fw.sb("lnb", [128, 1024]))
        qn_bc = B(fw.sb("qn_bc", [128, 64])); kn_bc = B(fw.sb("kn_bc", [128, 64]))
        glan_bc = B(fw.sb("glan_bc", [128, 256]))
        gw2b = B(fw.sb("gw2b_sb", [17, 2, 512]))
        skT = B(fw.sb("skT", [128, 2, 128]))
        xall = fw.sb("xall", [128, 8, 1024])
        xt = [B(xall[:, i, :], "x%d" % i) for i in range(8)]
        hTall = fw.sb("hTall", [128, 8, 2, 128])
        hT = [B(hTall[:, :, i, :], "hT%d" % i) for i in range(2)]
        NW = int(os.environ.get("NW", "3"))
        Wt = [B(fw.sb("W%d" % i, [128, 8, 256]), "W%d" % i) for i in range(NW)]
        w_ctr = [0]

        def nextW():
            w = Wt[w_ctr[0] % NW]
            w_ctr[0] += 1
            return w
        Wg = B(fw.sb("Wg", [128, 8, 32]))
        ARENA_N = int(os.environ.get("ARENA_N", "28000"))
        arena_t = fw.sb("arena", [128, ARENA_N])
        arena = Arena(arena_t, ARENA_N)
        psb = [Buf(fw.ps("ps%d" % i, [128, 512]), "ps%d" % i, psum=True) for i in range(8)]
        ps_ctr = [0]

        def PS():
            b = psb[ps_ctr[0] % 6]
            ps_ctr[0] += 1
            return b

        def load(q, buf, out_ap, in_ap):
            fw.dma(q, lambda e, o=out_ap, i=in_ap: e.dma_start(out=o, in_=i), buf, True)

        def store(q, buf, out_ap, in_ap):
            fw.dma(q, lambda e, o=out_ap, i=in_ap: e.dma_start(out=o, in_=i), buf, False)

        def mm(ob, o, lb, l, rb, r, start, stop, skip=False):
            op("pe", lambda e, o=o, l=l, r=r, s0=start, s1=stop, sk=skip: e.matmul(o, l, r, start=s0, stop=s1, skip_group_check=sk),
               [lb, rb], [ob])

        def tr(ob, o, ib, i):
            op("pe", lambda e, o=o, i=i: e.transpose(o, i, ident[:]), [ib, ident], [ob])

        def act(ob, o, ib, i, func, scale=1.0, bias=0.0, accum=None, accb=None, extra=()):
            if accum is None:
                op("act", lambda e, o=o, i=i, f=func, s=scale, b=bias: e.activation(out=o, in_=i, func=f, scale=s, bias=b),
                   [ib] + list(extra), [ob])
            else:
                op("act", lambda e, o=o, i=i, f=func, s=scale, b=bias, a=accum: e.activation(out=o, in_=i, func=f, scale=s, bias=b, accum_out=a),
                   [ib] + list(extra), [ob, accb])

        def tt(eng, ob, o, ab, a, bb, b, alu):
            op(eng, lambda e, o=o, a=a, b=b, alu=alu: e.tensor_tensor(out=o, in0=a, in1=b, op=alu), [ab, bb], [ob])

        def cp(eng, ob, o, ib, i):
            if eng == "act":
                op("act", lambda e, o=o, i=i: e.copy(out=o, in_=i), [ib], [ob])
            else:
                op(eng, lambda e, o=o, i=i: e.tensor_copy(out=o, in_=i), [ib], [ob])

        def ts(ob, o, ib, i, s1, s2, op0, op1, extra=()):
            op("dve", lambda e, o=o, i=i, s1=s1, s2=s2, op0=op0, op1=op1: e.tensor_scalar(out=o, in0=i, scalar1=s1, scalar2=s2, op0=op0, op1=op1),
               [ib] + list(extra), [ob])

        def stt(ob, o, ab, a, sc, bb, b, op0, op1, extra=(), accum=None, accb=None):
            if accum is None:
                op("dve", lambda e, o=o, a=a, sc=sc, b=b, op0=op0, op1=op1: e.scalar_tensor_tensor(out=o, in0=a, scalar=sc, in1=b, op0=op0, op1=op1),
                   [ab, bb] + list(extra), [ob])
            else:
                op("dve", lambda e, o=o, a=a, sc=sc, b=b, op0=op0, op1=op1, ac=accum: e.scalar_tensor_tensor(out=o, in0=a, scalar=sc, in1=b, op0=op0, op1=op1, accum_out=ac),
                   [ab, bb] + list(extra), [ob, accb])

        def memset(eng, ob, o, val):
            op(eng, lambda e, o=o, v=val: e.memset(o, v), [], [ob])

        def rstd_from(ssb, ss_ap, outb, out_ap, inv_n, eps):
            ts(outb, out_ap, ssb, ss_ap, inv_n, eps, ALU.mult, ALU.add)
            act(outb, out_ap, outb, out_ap, AF.Sqrt)
            op("dve", lambda e, o=out_ap: e.reciprocal(out=o, in_=o), [outb], [outb])

        load("sp", ident, ident[:], ident_d)
        load("sp", tri, tri[:], tri_d)
        load("sp", cind, cind[:], cind_d)
        load("sp", rope, rope[:], rope_d)
        load("sp", iota, iota[:], iota_d)
        load("sp", scond, scond[:], condT_d)
        load("sp", adab, adab[:], ada_bT_d)
        act(scond, scond[:], scond, scond[:], AF.Silu)

        if dbg == -1:
            store("sp", scond, yc_d[0, 0:128, 0:16], scond[:].rearrange("p c j -> p (c j)"))
            fw.barrier(); fw.run_block()
            return nc
        for l in range(depth):
            pm = PS()
            for blk in range(int(os.environ.get("NBLK", "24"))):
                w = nextW()
                load("sp", w, w[:], ada_w_d[l, :, blk * 256:(blk + 1) * 256].rearrange("(c p) n -> p c n", p=128))
                for c in range(2):
                    cg = blk * 2 + c
                    if os.environ.get("MODV") == "1":
                        continue
                    for kc in range(8):
                        mm(pm, pm[:, cg * 2:cg * 2 + 2], w, w[:, kc, c * 128:(c + 1) * 128], scond, scond[:, kc, :], kc == 0, kc == 7)
            for j in range(2):
                tt("dve", modT, modT[:, l, j, :], pm, pm[:, 0:96].rearrange("p (c j) -> p c j", j=2)[:, :, j], adab, adab[:, l, :], ALU.add)
            for j in range(2):
                for c0 in (8, 32):
                    ts(modT, modT[:, l, j, c0:c0 + 8], modT, modT[:, l, j, c0:c0 + 8], 1.0, None, ALU.add, ALU.bypass)

        if dbg == 0:
            store("sp", modT, yc_d[0, 0:128, 0:192], modT[:].rearrange("p l j c -> p (l j c)"))
            fw.barrier(); fw.run_block()
            return nc
        def make_hT(xb, slot, l, job, which):
            sh0 = 0 if which == 1 else 24
            sc0 = 8 if which == 1 else 32
            h = hT[slot]
            for half in range(2):
                p = PS()
                for c in range(4):
                    cc = half * 4 + c
                    tr(p, p[:, c * 128:(c + 1) * 128], xb, xb[:, cc * 128:(cc + 1) * 128])
                sc_b = modT[:, l, job, sc0 + half * 4:sc0 + half * 4 + 4].unsqueeze(2).to_broadcast([128, 4, 128])
                sh_b = modT[:, l, job, sh0 + half * 4:sh0 + half * 4 + 4].unsqueeze(2).to_broadcast([128, 4, 128])
                tt("dve", h, h[:, half * 4:half * 4 + 4, :], p, p[:].rearrange("p (c t) -> p c t", c=4), modT, sc_b, ALU.mult)
                tt("dve", h, h[:, half * 4:half * 4 + 4, :], h, h[:, half * 4:half * 4 + 4, :], modT, sh_b, ALU.add)

        def proj(slot, w, ncols, pb, pout, wcol0=0):
            for kc in range(8):
                mm(pb, pout, hT[slot], hT[slot][:, kc, :], w, w[:, kc, wcol0:wcol0 + ncols], kc == 0, kc == 7)

        def wload(w, dram2d, col0, ncols=256):
            load("sp", w, w[:, :, 0:ncols], dram2d[:, col0:col0 + ncols].rearrange("(c p) n -> p c n", p=128))

        def gate_bcast(l, job, c0):
            gB = B(arena.take(1024).rearrange("p (c m) -> p c m", c=8), "gB")
            cp("dve", gB, gB[:], modT, modT[:, l, job, c0:c0 + 8].unsqueeze(2).to_broadcast([128, 8, 128]))
            for half in range(2):
                p = PS()
                for c in range(4):
                    mm(p, p[:, c * 128:(c + 1) * 128], gB, gB[:, half * 4 + c, :], ident, ident[:], True, True)
                cp("act", gate_bc, gate_bc[:, half * 512:(half + 1) * 512], p, p[:])

        def layer_norm(preb, pre_ap, outb, out_ap, small):
            st6 = small["st6"]; mv = small["mv"]; rs = small["rs"]
            for c in range(2):
                op("dve", lambda e, o=st6[:, c * 6:(c + 1) * 6], i=pre_ap[:, c * 512:(c + 1) * 512]: e.bn_stats(out=o, in_=i), [preb], [st6])
            op("dve", lambda e: e.bn_aggr(out=mv[:], in_=st6[:]), [st6], [mv])
            ts(rs, rs[:], mv, mv[:, 1:2], 1.0, LN_EPS, ALU.mult, ALU.add)
            act(rs, rs[:], rs, rs[:], AF.Sqrt)
            op("dve", lambda e: e.reciprocal(out=rs[:], in_=rs[:]), [rs], [rs])
            ts(outb, out_ap, preb, pre_ap, mv[:, 0:1], rs[:, 0:1], ALU.subtract, ALU.mult, extra=[mv, rs])
            tt("dve", outb, out_ap, outb, out_ap, lng, lng[:], ALU.mult)
            tt("dve", outb, out_ap, outb, out_ap, lnb, lnb[:], ALU.add)

        def gstop(n):
            if os.environ.get("GSTOP") == str(n):
                raise StopIteration

        def run_unit(is_sample, useq):
            if not is_sample and not isinstance(useq, (list, tuple)):
                useq = [useq]
            NT = 8 if is_sample else 2 * len(useq)
            NKC = 12 if is_sample else 2
            job = 1 if is_sample else 0
            y_dst = ys_d if is_sample else yc_d[useq[0]]

            def xrow(t):
                if is_sample:
                    return xs_d[t * 128:(t + 1) * 128, :]
                return xc_d[useq[t // 2]][(t % 2) * 128:(t % 2 + 1) * 128, :]

            def yrow(t):
                if is_sample:
                    return ys_d[t * 128:(t + 1) * 128, :]
                return yc_d[useq[t // 2]][(t % 2) * 128:(t % 2 + 1) * 128, :]
            for l in range(depth):
                win = w_in_d[l]
                last_half = is_sample and (l == depth - 1)
                NOWN = NT // 2 if last_half else NT
                fw.barrier()
                arena.reset()
                kT = B(arena.take(2 * 1536).rearrange("p (j k) -> p j k", j=2), "kT")
                vA = B(arena.take(12 * 4 * 65).rearrange("p (c g d) -> p c g d", c=12, g=4), "vA")
                o_all = [B(arena.take(1024), "o%d" % i) for i in range(NT)]
                glr_sb = B(arena.take(NT * 32).rearrange("p (t c) -> p t c", c=32), "glr")
                small = {"st6": B(arena.take(12)), "mv": B(arena.take(2)), "rs": B(arena.take(1))}
                mark_persist = arena.off
                load("sp", qn_bc, qn_bc[:], qn_d[l:l + 1, :].partition_broadcast(128) if False else qn_d[l].partition_broadcast(128))
                load("sp", kn_bc, kn_bc[:], kn_d[l].partition_broadcast(128))
                load("sp", glan_bc, glan_bc[:], glan_d[l].partition_broadcast(128))
                load("sp", gw2b, gw2b[:], (gw2b_s_d if is_sample else gw2b_c_d)[l])
                load("sp", lng, lng[:], ln1g_d[l].partition_broadcast(128))
                load("sp", lnb, lnb[:], ln1b_d[l].partition_broadcast(128))
                memset("dve", vA, vA[:], 1.0)

                wk_ = nextW(); wv2_ = nextW()
                wload(wk_, win, 1024); wload(wv2_, win, 1280)
                load("sp", Wg, Wg[:], (wglr_s_d if is_sample else wglr_c_d)[l].rearrange("(c p) n -> p c n", p=128))
                t_sq = B(arena.take(256)); t_ss = B(arena.take(4)); t_kn = B(arena.take(256)); t_kr = B(arena.take(256))
                t_a = B(arena.take(128)); t_b = B(arena.take(128)); t_v = B(arena.take(256)); t_kp = B(arena.take(256))
                for t in range(NT):
                    if l == 0:
                        load("sp", xt[t], xt[t][:], xrow(t))
                    make_hT(xt[t], 0, l, job, 1)
                    pk = PS(); proj(0, wk_, 256, pk, pk[:, 0:256])
                    pv = PS(); proj(0, wv2_, 256, pv, pv[:, 0:256])
                    for kc in range(8):
                        mm(pv, pv[:, 256:288], hT[0], hT[0][:, kc, :], Wg, Wg[:, kc, :], kc == 0, kc == 7)
                    act(t_sq, t_sq[:], pk, pk[:, 0:256], AF.Square)
                    op("dve", lambda e: e.tensor_reduce(out=t_ss[:], in_=t_sq[:].rearrange("p (h d) -> p h d", d=64), op=ALU.add, axis=AX.X), [t_sq], [t_ss])
                    rstd_from(t_ss, t_ss[:], t_ss, t_ss[:], 1.0 / 64, RMS_EPS)
                    tt("dve", t_kn, t_kn[:].rearrange("p (h d) -> p h d", d=64), pk, pk[:, 0:256].rearrange("p (h d) -> p h d", d=64),
                       t_ss, t_ss[:].unsqueeze(2).to_broadcast([128, 4, 64]), ALU.mult)
                    tt("dve", t_kn, t_kn[:].rearrange("p (h d) -> p h d", d=64), t_kn, t_kn[:].rearrange("p (h d) -> p h d", d=64),
                       kn_bc, kn_bc[:].unsqueeze(1).to_broadcast([128, 4, 64]), ALU.mult)
                    cp("act", t_v, t_v[:], pv, pv[:, 0:256])
                    cp("act", glr_sb, glr_sb[:, t, :], pv, pv[:, 256:288])
                    if not is_sample:
                        store("sp", t_kn, nk_d[useq[t // 2], l, (t % 2) * 128:(t % 2 + 1) * 128, :], t_kn[:])
                        store("sp", t_v, nv_d[useq[t // 2], l, (t % 2) * 128:(t % 2 + 1) * 128, :], t_v[:])
                        ksrc = t_kn
                    else:
                        do_rope(t_kn, t_kr, 4, t, t_a, t_b)
                        ksrc = t_kr
                    cp("dve", vA, vA[:, t, :, 0:64], t_v, t_v[:].rearrange("p (g d) -> p g d", d=64))
                    cp("dve", t_kp, t_kp[:].rearrange("p (j hi d) -> p hi j d", j=2, hi=2), ksrc, ksrc[:].rearrange("p (hi j d) -> p hi j d", hi=2, j=2))
                    pt = PS()
                    for j in range(2):
                        tr(pt, pt[:, j * 128:(j + 1) * 128], t_kp, t_kp[:, j * 128:(j + 1) * 128])
                        cp("act", kT, kT[:, j, t * 128:(t + 1) * 128], pt, pt[:, j * 128:(j + 1) * 128])
                if is_sample:
                    for i in range(4):
                        load("sp", t_kr, t_kr[:], ck_d[l, i * 128:(i + 1) * 128, :])
                        cp("dve", t_kp, t_kp[:].rearrange("p (j hi d) -> p hi j d", j=2, hi=2), t_kr, t_kr[:].rearrange("p (hi j d) -> p hi j d", hi=2, j=2))
                        pt = PS()
                        for j in range(2):
                            tr(pt, pt[:, j * 128:(j + 1) * 128], t_kp, t_kp[:, j * 128:(j + 1) * 128])
                            cp("act", kT, kT[:, j, 1024 + i * 128:1024 + (i + 1) * 128], pt, pt[:, j * 128:(j + 1) * 128])
                        load("sp", vA, vA[:, 8 + i, :, 0:64], cv_d[l, i * 128:(i + 1) * 128, :].rearrange("p (g d) -> p g d", d=64))

                if dbg == 1:
                    raise StopIteration
                fw.barrier()
                arena.off = mark_persist
                gq_s = B(arena.take(NT * 128).rearrange("p (t c) -> p t c", c=128), "gq")
                gk_s = B(arena.take(NT * 128).rearrange("p (t c) -> p t c", c=128), "gk")
                gv_s = [B(arena.take(256), "gv%d" % i) for i in range(NT)]
                sp_s = [B(arena.take(256).rearrange("p (d c) -> p d c", d=2), "sp%d" % i) for i in range(NT)]
                glrT1 = B(arena.take(256, parts=17).rearrange("p (d t) -> p d t", d=2), "glrT1")
                t_e = B(arena.take(256))
                Eb = [[B(arena.take(128)) for _ in range(3)] for _ in range(2)]
                dec = [B(arena.take(2)) for _ in range(2)]
                qe = [B(arena.take(128)) for _ in range(2)]; ke = [B(arena.take(128)) for _ in range(2)]; kd = [B(arena.take(128)) for _ in range(2)]
                qlo = [B(arena.take(128)) for _ in range(2)]; qhi = [B(arena.take(128)) for _ in range(2)]; keT = [B(arena.take(128)) for _ in range(2)]
                ATs = [B(arena.take(128)) for _ in range(2)]
                Sb = [B(arena.take(256), "S%d" % i) for i in range(3)]
                for bq in qlo + qhi:
                    memset("dve", bq, bq[:], 0.0)
                memset("dve", glrT1, glrT1[:], 1.0)
                for h in range(4):
                    wa_ = nextW(); wb_ = nextW()
                    load("sp", wa_, wa_[:, :, 0:128], win[:, 1536 + h * 128:1536 + (h + 1) * 128].rearrange("(c p) n -> p c n", p=128))
                    load("sp", wa_, wa_[:, :, 128:256], win[:, 2048 + h * 128:2048 + (h + 1) * 128].rearrange("(c p) n -> p c n", p=128))
                    wload(wb_, win, 2560 + h * 256)
                    for t in range(NT):
                        make_hT(xt[t], 0, l, job, 1)
                        p1 = PS(); proj(0, wa_, 256, p1, p1[:, 0:256])
                        p2 = PS(); proj(0, wb_, 256, p2, p2[:, 0:256])
                        cp("act", gq_s, gq_s[:, t, :], p1, p1[:, 0:128])
                        cp("act", gk_s, gk_s[:, t, :], p1, p1[:, 128:256])
                        cp("dve", gv_s[t], gv_s[t][:], p2, p2[:, 0:256])
                        pz = PS()
                        for d in range(2):
                            tr(pz, pz[0:16, d * 128:(d + 1) * 128], glr_sb, glr_sb[:, t, d * 16:(d + 1) * 16])
                        cp("dve", glrT1, glrT1[0:16, :, :], pz, pz[0:16, 0:256].rearrange("p (d t) -> p d t", d=2))
                        for d in range(2):
                            mm(pz, pz[:, 256 + d * 128:256 + (d + 1) * 128], glrT1, glrT1[0:17, d, :], gw2b, gw2b[0:17, d, h * 128:(h + 1) * 128], True, True)
                        act(t_e, t_e[:], pz, pz[:, 256:512], AF.Exp, scale=-1.0)
                        act(sp_s[t], sp_s[t][:].rearrange("p d c -> p (d c)"), t_e, t_e[:], AF.Ln, bias=1.0)
                    if os.environ.get("GSTOP") == "1":
                        raise StopIteration
                    scan_jobs = [(None, d) for d in range(2)] if is_sample else [(j, d) for j in range(len(useq)) for d in range(2)]
                    for (sj, d) in scan_jobs:
                        if is_sample:
                            load("sp", Sb[0], Sb[0][:], sg_d[l, d, h])
                            order = list(range(NOWN)) if d == 0 else list(range(NT - 1, -1, -1))
                        else:
                            memset("dve", Sb[0], Sb[0][:], 0.0)
                            order = [2 * sj, 2 * sj + 1] if d == 0 else [2 * sj + 1, 2 * sj]
                        si = 0
                        for it, t in enumerate(order):
                            z = it % 2
                            c1, c2 = (0, 1) if d == 0 else (1, 0)
                            S0, S1, S2 = Sb[si % 3], Sb[(si + 1) % 3], Sb[(si + 2) % 3]
                            si += 2
                            spb = sp_s[t]; spa = sp_s[t][:, d, :]
                            pa = PS()
                            mm(pa, pa[:, 0:128], tri, tri[:, d, :], spb, spa, True, True)
                            mm(pa, pa[:, 128:256], tri, tri[:, 2 + d, :], spb, spa, True, True)
                            mm(pa, pa[:, 256:258], spb, spa, cind, cind[:], True, True)
                            E, Ei, Dd = Eb[z]
                            act(E, E[:], pa, pa[:, 0:128], AF.Exp)
                            act(Ei, Ei[:], pa, pa[:, 0:128], AF.Exp, scale=-1.0)
                            act(Dd, Dd[:], pa, pa[:, 128:256], AF.Exp)
                            act(dec[z], dec[z][:], pa, pa[:, 256:258], AF.Exp)
                            gstop(2)
                            stt(qe[z], qe[z][:], gq_s, gq_s[:, t, :], 128.0 ** -0.5, E, E[:], ALU.mult, ALU.mult)
                            tt("dve", ke[z], ke[z][:], gk_s, gk_s[:, t, :], Ei, Ei[:], ALU.mult)
                            tt("dve", kd[z], kd[z][:], gk_s, gk_s[:, t, :], Dd, Dd[:], ALU.mult)
                            gstop(21)
                            pb = PS()
                            tr(pb, pb[:, 0:128], qe[z], qe[z][:])
                            tr(pb, pb[:, 128:256], ke[z], ke[z][:])
                            gstop(22)
                            cp("act", qlo[z], qlo[z][:, 0:64], pb, pb[:, 0:64])
                            gstop(23)
                            cp("act", qhi[z], qhi[z][:, 64:128], pb, pb[:, 64:128])
                            gstop(24)
                            cp("act", keT[z], keT[z][:], pb, pb[:, 128:256])
                            gstop(3)
                            need_o = t < NOWN
                            if need_o:
                                mm(pb, pb[:, 256:384], keT[z], keT[z][:], qlo[z], qlo[z][:], True, False)
                                mm(pb, pb[:, 256:384], keT[z], keT[z][:], qhi[z], qhi[z][:], False, True)
                                tt("dve", ATs[z], ATs[z][:], pb, pb[:, 256:384], tri, tri[:, 4 + d, :], ALU.mult)
                            gstop(4)
                            pc = PS(); pc2 = PS()
                            qhalf = {0: qlo[z], 1: qhi[z]}
                            mm(pc, pc[:, 0:256], kd[z], kd[z][c1 * 64:(c1 + 1) * 64, :], gv_s[t], gv_s[t][c1 * 64:(c1 + 1) * 64, :], True, True)
                            stt(S1, S1[:], S0, S0[:], dec[z][:, c1:c1 + 1], pc, pc[:, 0:256], ALU.mult, ALU.add, extra=[dec[z]])
                            mm(pc2, pc2[:, 0:256], kd[z], kd[z][c2 * 64:(c2 + 1) * 64, :], gv_s[t], gv_s[t][c2 * 64:(c2 + 1) * 64, :], True, True)
                            stt(S2, S2[:], S1, S1[:], dec[z][:, c2:c2 + 1], pc2, pc2[:, 0:256], ALU.mult, ALU.add, extra=[dec[z]])
                            gstop(5)
                            if need_o:
                                po = PS()
                                mm(po, po[:, 0:256], ATs[z], ATs[z][:], gv_s[t], gv_s[t][:], True, False)
                                mm(po, po[:, 0:256], qhalf[c1], qhalf[c1][:], S0, S0[:], False, False)
                                mm(po, po[:, 0:256], qhalf[c2], qhalf[c2][:], S1, S1[:], False, True)
                                oo = o_all[t]
                                if d == 0:
                                    cp("act", oo, oo[:, h * 256:(h + 1) * 256], po, po[:, 0:256])
                                else:
                                    tt("dve", oo, oo[:, h * 256:(h + 1) * 256], oo, oo[:, h * 256:(h + 1) * 256], po, po[:, 0:256], ALU.add)
                            gstop(6)
                        gstop(7)
                        Sf = Sb[si % 3]
                        if not is_sample:
                            store("sp", Sf, ns_d[useq[sj], l, d, h], Sf[:])

                if dbg == 2:
                    raise StopIteration
                fw.barrier()
                arena.off = mark_persist
                gate_bcast(l, job, 16)
                q_nf = arena.take(384); q_rf = arena.take(384)
                q_n = B(q_nf.rearrange("p (h d) -> p h d", d=64), "q_n")
                q_r = B(q_rf.rearrange("p (h d) -> p h d", d=64), "q_r")
                q_sq = B(arena.take(256)); q_ss = B(arena.take(4)); q_a = B(arena.take(128)); q_b = B(arena.take(128))
                qT = B(arena.take(512), "qT")
                Pex = [B(arena.take(512)) for _ in range(2)]
                rec = B(arena.take(4))
                attn = [B(arena.take(1024), "attn%d" % i) for i in range(2)]
                ya = [B(arena.take(1024), "ya%d" % i) for i in range(2)]
                Tb = [B(arena.take(1024).rearrange("p (c t) -> p c t", c=8), "Tb%d" % i) for i in range(2)]
                gs = B(arena.take(256)); gtmp = B(arena.take(256)); g_ss = B(arena.take(1)); g_sq = B(arena.take(256))
                memset("dve", q_n, q_n[:], 0.0)
                memset("dve", q_r, q_r[:], 0.0)
                for g0 in range(NOWN // 2):
                    tiles = [2 * g0, 2 * g0 + 1]
                    for s in range(2):
                        make_hT(xt[tiles[s]], s, l, job, 1)
                    for b in range(4):
                        wq_ = nextW()
                        wload(wq_, win, b * 256)
                        half = b // 2; jj = b % 2
                        prt = slice(half * 64, half * 64 + 64)
                        for s in range(2):
                            tg = tiles[s]
                            pq = PS(); proj(s, wq_, 256, pq, pq[:, 0:256])
                            act(q_sq, q_sq[:], pq, pq[:, 0:256], AF.Square)
                            op("dve", lambda e: e.tensor_reduce(out=q_ss[:], in_=q_sq[:].rearrange("p (h d) -> p h d", d=64), op=ALU.add, axis=AX.X), [q_sq], [q_ss])
                            rstd_from(q_ss, q_ss[:], q_ss, q_ss[:], 1.0 / 64, RMS_EPS)
                            tt("dve", q_n, q_n[:, 1:5, :], pq, pq[:, 0:256].rearrange("p (h d) -> p h d", d=64),
                               q_ss, q_ss[:].unsqueeze(2).to_broadcast([128, 4, 64]), ALU.mult)
                            tt("dve", q_n, q_n[:, 1:5, :], q_n, q_n[:, 1:5, :], qn_bc, qn_bc[:].unsqueeze(1).to_broadcast([128, 4, 64]), ALU.mult)
                            if is_sample:
                                do_rope(q_n, q_r, 4, tg, q_a, q_b, pad=1)
                                qs = q_r; qsf = q_rf
                            else:
                                qs = q_n; qsf = q_nf
                            pt = PS()
                            for r in range(4):
                                if half == 0:
                                    tr(pt, pt[:, r * 128:(r + 1) * 128], qs, qsf[:, (r + 1) * 64:(r + 3) * 64])
                                else:
                                    tr(pt, pt[:, r * 128:(r + 1) * 128], qs, qsf[:, r * 64:(r + 2) * 64])
                            cp("act", qT, qT[prt, :], pt, pt[prt, :])
                            pvb = psb[7]
                            kcs = list(range(NKC)) if is_sample else [2 * g0, 2 * g0 + 1]
                            for ik, kc in enumerate(kcs):
                                psc = PS()
                                mm(psc, psc[:, :], kT, kT[prt, jj, kc * 128:(kc + 1) * 128], qT, qT[prt, :], True, True)
                                pe_ = Pex[ik % 2]
                                act(pe_, pe_[:], psc, psc[:, :], AF.Exp, scale=0.125)
                                for r in range(4):
                                    mm(pvb, pvb[:, r * 65:(r + 1) * 65], pe_, pe_[:, r * 128:(r + 1) * 128], vA, vA[:, kc, b, :],
                                       (ik == 0 and r == 0), (ik == len(kcs) - 1 and r == 3), skip=True)
                            pv3 = pvb[:, 0:260].rearrange("p (r d) -> p r d", d=65)
                            op("dve", lambda e, o=rec[:], i=pv3[:, :, 64]: e.reciprocal(out=o, in_=i), [pvb], [rec])
                            tt("dve", attn[s], attn[s][:, b * 256:(b + 1) * 256].rearrange("p (r d) -> p r d", d=64), pvb, pv3[:, :, 0:64],
                               rec, rec[:].unsqueeze(2).to_broadcast([128, 4, 64]), ALU.mult)
                    if dbg == 4:
                        store("sp", attn[0], y_dst[0:128, :], attn[0][:])
                        store("sp", attn[1], y_dst[128:256, :], attn[1][:])
                        store("sp", o_all[0], y_dst[256:384, :], o_all[0][:])
                        store("sp", o_all[NT - 1], y_dst[384:512, :], o_all[NT - 1][:])
                        raise StopIteration
                    for b in range(4):
                        wg_ = nextW()
                        wload(wg_, win, 3584 + b * 256)
                        for s in range(2):
                            oo = o_all[tiles[s]]
                            osl = oo[:, b * 256:(b + 1) * 256]
                            pg = PS(); proj(s, wg_, 256, pg, pg[:, 0:256])
                            act(gtmp, gtmp[:], pg, pg[:, 0:256], AF.Silu)
                            act(g_sq, g_sq[:], oo, osl, AF.Square, accum=g_ss[:], accb=g_ss)
                            rstd_from(g_ss, g_ss[:], g_ss, g_ss[:], 1.0 / 256, RMS_EPS)
                            stt(oo, osl, oo, osl, g_ss[:, 0:1], glan_bc, glan_bc[:], ALU.mult, ALU.mult, extra=[g_ss])
                            tt("dve", oo, osl, oo, osl, gtmp, gtmp[:], ALU.mult)
                    for s in range(2):
                        for half in range(2):
                            p = PS()
                            for c in range(4):
                                tr(p, p[:, c * 128:(c + 1) * 128], attn[s], attn[s][:, (half * 4 + c) * 128:(half * 4 + c + 1) * 128])
                            cp("act", Tb[s], Tb[s][:, half * 4:half * 4 + 4, :], p, p[:].rearrange("p (c t) -> p c t", c=4))
                    for b in range(4):
                        w0_ = nextW(); w1_ = nextW()
                        wload(w0_, wao_d[l], b * 256)
                        wload(w1_, win, 4640 + b * 256)
                        for s in range(2):
                            pgm = PS(); proj(s, w1_, 256, pgm, pgm[:, 0:256])
                            act(gs, gs[:], pgm, pgm[:, 0:256], AF.Sigmoid)
                            py = PS()
                            for kc in range(8):
                                mm(py, py[:, 0:256], Tb[s], Tb[s][:, kc, :], w0_, w0_[:, kc, :], kc == 0, kc == 7)
                            tt("dve", ya[s], ya[s][:, b * 256:(b + 1) * 256], py, py[:, 0:256], gs, gs[:], ALU.mult)
                    for s in range(2):
                        oo = o_all[tiles[s]]
                        for half in range(2):
                            p = PS()
                            for c in range(4):
                                tr(p, p[:, c * 128:(c + 1) * 128], oo, oo[:, (half * 4 + c) * 128:(half * 4 + c + 1) * 128])
                            cp("act", Tb[s], Tb[s][:, half * 4:half * 4 + 4, :], p, p[:].rearrange("p (c t) -> p c t", c=4))
                    for b in range(4):
                        w0_ = nextW(); w1_ = nextW()
                        wload(w0_, wgo_d[l], b * 256)
                        wload(w1_, win, 4640 + 1024 + b * 256)
                        for s in range(2):
                            pgm = PS(); proj(s, w1_, 256, pgm, pgm[:, 0:256])
                            act(gs, gs[:], pgm, pgm[:, 0:256], AF.Sigmoid)
                            py = PS()
                            for kc in range(8):
                                mm(py, py[:, 0:256], Tb[s], Tb[s][:, kc, :], w0_, w0_[:, kc, :], kc == 0, kc == 7)
                            tt("dve", gtmp, gtmp[:], py, py[:, 0:256], gs, gs[:], ALU.mult)
                            tt("dve", ya[s], ya[s][:, b * 256:(b + 1) * 256], ya[s], ya[s][:, b * 256:(b + 1) * 256], gtmp, gtmp[:], ALU.add)
                    for s in range(2):
                        for half in range(2):
                            p = PS()
                            for c in range(4):
                                tr(p, p[:, c * 128:(c + 1) * 128], ya[s], ya[s][:, (half * 4 + c) * 128:(half * 4 + c + 1) * 128])
                            cp("act", Tb[s], Tb[s][:, half * 4:half * 4 + 4, :], p, p[:].rearrange("p (c t) -> p c t", c=4))
                    for b in range(4):
                        wo_ = nextW()
                        wload(wo_, wout_d[l], b * 256)
                        for s in range(2):
                            py = PS()
                            for kc in range(8):
                                mm(py, py[:, 0:256], Tb[s], Tb[s][:, kc, :], wo_, wo_[:, kc, :], kc == 0, kc == 7)
                            tt("dve", attn[s], attn[s][:, b * 256:(b + 1) * 256], py, py[:, 0:256], gate_bc, gate_bc[:, b * 256:(b + 1) * 256], ALU.mult)
                    for s in range(2):
                        xb = xt[tiles[s]]
                        stt(attn[s], attn[s][:], xb, xb[:], ALPHA, attn[s], attn[s][:], ALU.mult, ALU.add)
                        layer_norm(attn[s], attn[s][:], xb, xb[:], small)

                if dbg == 3:
                    for t in range(NT):
                        store("sp", xt[t], yrow(t), xt[t][:])
                    raise StopIteration
                fw.barrier()
                arena.reset()
                small = {"st6": B(arena.take(12)), "mv": B(arena.take(2)), "rs": B(arena.take(1))}
                load("sp", lng, lng[:], ln2g_d[l].partition_broadcast(128))
                load("sp", lnb, lnb[:], ln2b_d[l].partition_broadcast(128))
                gate_bcast(l, job, 40)
                if do_peer:
                    peer_phase(l, job, NOWN, small)
                else:
                    for t in range(NOWN):
                        pre = B(arena.take(1024)) if t == 0 else pre
                        op("dve", lambda e, o=pre[:], i=xt[t][:]: e.tensor_scalar(out=o, in0=i, scalar1=ALPHA, scalar2=None, op0=ALU.mult, op1=ALU.bypass), [xt[t]], [pre])
                        layer_norm(pre, pre[:], xt[t], xt[t][:], small)
                if l == depth - 1:
                    for t in range(NOWN):
                        store("sp", xt[t], yrow(t), xt[t][:])

        def do_rope(src, dst, H, tile_idx, ta, tb, pad=0):
            if pad:
                s5 = src[:, pad:pad + H, :].rearrange("p h (a f r) -> p h a f r", a=2, f=2)
                d5 = dst[:, pad:pad + H, :].rearrange("p h (a f r) -> p h a f r", a=2, f=2)
            else:
                s5 = src[:].rearrange("p (h a f r) -> p h a f r", h=H, a=2, f=2)
                d5 = dst[:].rearrange("p (h a f r) -> p h a f r", h=H, a=2, f=2)
            cos = rope[:, tile_idx, 0, :].rearrange("p (a r) -> p a r", a=2).unsqueeze(1).to_broadcast([128, H, 2, 16])
            sin = rope[:, tile_idx, 1, :].rearrange("p (a r) -> p a r", a=2).unsqueeze(1).to_broadcast([128, H, 2, 16])
            x1 = s5[:, :, :, 0, :]; x2 = s5[:, :, :, 1, :]
            a4 = ta[:].rearrange("p (h a r) -> p h a r", h=H, a=2)
            b4 = tb[:].rearrange("p (h a r) -> p h a r", h=H, a=2)
            tt("dve", ta, a4, src, x1, rope, cos, ALU.mult)
            tt("dve", tb, b4, src, x2, rope, sin, ALU.mult)
            tt("dve", dst, d5[:, :, :, 0, :], ta, a4, tb, b4, ALU.subtract)
            tt("dve", ta, a4, src, x2, rope, cos, ALU.mult)
            tt("dve", tb, b4, src, x1, rope, sin, ALU.mult)
            tt("dve", dst, d5[:, :, :, 1, :], ta, a4, tb, b4, ALU.add)

        PV_ = os.environ.get("PEERV", "")
        PEMOD = int(os.environ.get("PEMOD", "1000"))
        RATE1 = int(os.environ.get("RATE1", "2")); RATE2 = int(os.environ.get("RATE2", "4"))

        def peer_phase(l, job, NT, small):
            t_sk = B(arena.take(128))
            for p_ in range(2):
                load("sp", t_sk, t_sk[:], sk_d[l, p_])
                pp = PS()
                tr(pp, pp[:, 0:128], t_sk, t_sk[:])
                cp("act", skT, skT[:, p_, :], pp, pp[:, 0:128])
            qpT = B(arena.take(16 * 256).rearrange("p (q s t) -> p q s t", q=16, s=2), "qpT")
            s_sb = B(arena.take(2048).rearrange("p (q n) -> p q n", q=16), "s_sb")
            s_wk = B(arena.take(2048).rearrange("p (q n) -> p q n", q=16), "s_wk")
            tv = B(arena.take(256).rearrange("p (q k) -> p q k", q=16), "tv")
            ti = B(arena.take(256).rearrange("p (q k) -> p q k", q=16), "ti")
            ti_t = arena_t[:, arena.off - 256:arena.off].bitcast(U32).rearrange("p (q k) -> p q k", q=16)
            tif = B(arena.take(256).rearrange("p (q k) -> p q k", q=16), "tif")
            cand = Buf(s_wk[:].rearrange("p q n -> p (q n)").rearrange("p (h c) -> p h c", h=8), "cand", base=s_wk)
            cwk = Buf(s_sb[:].rearrange("p q n -> p (q n)").rearrange("p (h c) -> p h c", h=8), "cwk", base=s_sb)
            sc = B(arena.take(128).rearrange("p (h k) -> p h k", h=8), "sc")
            ci = B(arena.take(128), "ci")
            ci_u = arena_t[:, arena.off - 128:arena.off].bitcast(U32).rearrange("p (h k) -> p h k", h=8)
            ca = B(arena.take(128), "ca"); ca_u = arena_t[:, arena.off - 128:arena.off].bitcast(U32)
            cb = B(arena.take(128), "cb"); cb_u = arena_t[:, arena.off - 128:arena.off].bitcast(U32)
            caf = B(arena.take(128).rearrange("p (h k) -> p h k", h=8), "caf")
            cbf = B(arena.take(128).rearrange("p (h k) -> p h k", h=8), "cbf")
            oh = cwk
            oh4 = cwk[:].rearrange("p h (k a) -> p h k a", k=16)
            i1s = B(arena.take(128).rearrange("p (h k) -> p h k", h=8), "i1s")
            i2s = B(arena.take(128).rearrange("p (h k) -> p h k", h=8), "i2s")
            idxf = B(arena.take(128), "idxf")
            idxu2, idxu2_u, gw2, h2b2 = [], [], [], []
            for i in range(2):
                idxu2.append(B(arena.take(128), "idxu%d" % i)); idxu2_u.append(arena_t[:, arena.off - 128:arena.off].bitcast(U32))
                gw2.append(B(arena.take(128).rearrange("p (h k) -> p h k", h=8), "gw%d" % i))
                h2b2.append(B(arena.take(1024), "h2b%d" % i))
            gsum = B(arena.take(8), "gsum")
            av = B(arena.take(128), "av")
            wv = B(arena.take(128), "wv")
            NB = int(os.environ.get("NBUF", "6"))
            GB = [B(arena.take(2048), "G%d" % i) for i in range(NB)]
            gctr = [0]
            NR = 8
            av_r = [B(arena.take(1), "av%d" % i) for i in range(NR)]
            wt_r = [B(arena.take(1), "wt%d" % i) for i in range(NR)]
            wv_r = [B(arena.take(1), "wv%d" % i) for i in range(NR)]
            DG = [B(arena.take(128), "DG%d" % i) for i in range(4)]
            dgc = [0]
            oth = B(arena.take(1024), "oth")
            junk = oth
            if os.environ.get("ARENA_DBG"):
                print("PEER arena used", arena.off, "of", arena.size)
            nseg = 16
            NG = NT // 2

            def stageA(g0):
                tiles = [2 * g0, 2 * g0 + 1]
                for s in range(2):
                    make_hT(xt[tiles[s]], s, l, job, 2)
                    yield
                for blk in range(8):
                    w = nextW()
                    wload(w, wq_d[l], blk * 256)
                    for qq in range(2):
                        hp = blk * 2 + qq
                        pq = PS()
                        for kc in range(8):
                            op("pe", lambda e, o=pq[:, 0:256], lw=w[:, kc, qq * 128:(qq + 1) * 128], r=hTall[:, kc, :, :].rearrange("p s t -> p (s t)"), s0=(kc == 0), s1=(kc == 7):
                               e.matmul(o, lw, r, start=s0, stop=s1), [w, hT[0], hT[1]], [pq])
                            if kc == 3:
                                yield
                        yield
                        cp("act", qpT, qpT[:, hp, :, :], pq, pq[:, 0:256].rearrange("p (s t) -> p s t", s=2))
                        yield

            def stageB(g0, s):
                idxu = idxu2[s]; idxu_u = idxu2_u[s]; gw = gw2[s]; h2b = h2b2[s]
                for bq in range(4):
                    pscr = PS()
                    for q4 in range(4):
                        hp = bq * 4 + q4
                        mm(pscr, pscr[:, q4 * 128:(q4 + 1) * 128], qpT, qpT[:, hp, s, :], skT, skT[:, hp % 2, :], True, True)
                    cp("act", s_sb, s_sb[:, bq * 4:bq * 4 + 4, :], pscr, pscr[:].rearrange("p (q n) -> p q n", q=4))
                    yield
                for half in range(2):
                    p = PS()
                    for c in range(4):
                        tr(p, p[:, c * 128:(c + 1) * 128], hT[s], hT[s][:, half * 4 + c, :])
                    cp("act", h2b, h2b[:, half * 512:(half + 1) * 512], p, p[:])
                    yield
                for q in range(nseg):
                    op("dve", lambda e, o=tv[:, q, 0:8], i=s_sb[:, q, :]: e.max(out=o, in_=i), [s_sb], [tv])
                    yield
                    op("dve", lambda e, o=ti_t[:, q, 0:8], m=tv[:, q, 0:8], i=s_sb[:, q, :]: e.max_index(out=o, in_max=m, in_values=i), [s_sb, tv], [ti])
                    yield
                    op("dve", lambda e, o=s_wk[:, q, :], m=tv[:, q, 0:8], i=s_sb[:, q, :]: e.match_replace(out=o, in_to_replace=m, in_values=i, imm_value=NEG), [s_sb, tv], [s_wk])
                    yield
                    op("dve", lambda e, o=tv[:, q, 8:16], i=s_wk[:, q, :]: e.max(out=o, in_=i), [s_wk], [tv])
                    yield
                    op("dve", lambda e, o=ti_t[:, q, 8:16], m=tv[:, q, 8:16], i=s_wk[:, q, :]: e.max_index(out=o, in_max=m, in_values=i), [s_wk, tv], [ti])
                    yield
                cp("dve", tif, tif[:], ti, ti_t)
                tv4 = tv[:].rearrange("p (h q) k -> p h q k", q=2)
                tif4 = tif[:].rearrange("p (h q) k -> p h q k", q=2)
                c4 = cand[:].rearrange("p h (a b) -> p h a b", a=16)
                for h in range(8):
                    tt("dve", cand, c4[:, h, :, :], tv, tv4[:, h, 0, :].unsqueeze(2).to_broadcast([128, 16, 16]),
                       tv, tv4[:, h, 1, :].unsqueeze(1).to_broadcast([128, 16, 16]), ALU.add)
                    yield
                for h in range(8):
                    op("dve", lambda e, o=sc[:, h, 0:8], i=cand[:, h, :]: e.max(out=o, in_=i), [cand], [sc])
                    yield
                    op("dve", lambda e, o=ci_u[:, h, 0:8], m=sc[:, h, 0:8], i=cand[:, h, :]: e.max_index(out=o, in_max=m, in_values=i), [cand, sc], [ci])
                    yield
                    op("dve", lambda e, o=cwk[:, h, :], m=sc[:, h, 0:8], i=cand[:, h, :]: e.match_replace(out=o, in_to_replace=m, in_values=i, imm_value=NEG), [cand, sc], [cwk])
                    yield
                    op("dve", lambda e, o=sc[:, h, 8:16], i=cwk[:, h, :]: e.max(out=o, in_=i), [cwk], [sc])
                    yield
                    op("dve", lambda e, o=ci_u[:, h, 8:16], m=sc[:, h, 8:16], i=cwk[:, h, :]: e.max_index(out=o, in_max=m, in_values=i), [cwk, sc], [ci])
                    yield
                op("dve", lambda e: e.tensor_single_scalar(out=ca_u, in_=ci_u.rearrange("p h k -> p (h k)"), scalar=4, op=ALU.logical_shift_right), [ci], [ca])
                op("dve", lambda e: e.tensor_single_scalar(out=cb_u, in_=ci_u.rearrange("p h k -> p (h k)"), scalar=15, op=ALU.bitwise_and), [ci], [cb])
                yield
                cp("dve", caf, caf[:].rearrange("p h k -> p (h k)"), ca, ca_u)
                cp("dve", cbf, cbf[:].rearrange("p h k -> p (h k)"), cb, cb_u)
                yield
                io_b = iota[:].unsqueeze(1).to_broadcast([128, 16, 16])
                for (cf, col, dst) in ((caf, 0, i1s), (cbf, 1, i2s)):
                    for h in range(8):
                        tt("dve", oh, oh4[:, h, :, :], cf, cf[:, h, :].unsqueeze(2).to_broadcast([128, 16, 16]), iota, io_b, ALU.is_equal)
                        yield
                        tt("dve", oh, oh4[:, h, :, :], oh, oh4[:, h, :, :], tif, tif4[:, h, col, :].unsqueeze(1).to_broadcast([128, 16, 16]), ALU.mult)
                        yield
                    op("dve", lambda e, o=dst[:].rearrange("p h k -> p (h k)"), i=oh4.rearrange("p h k a -> p (h k) a"): e.tensor_reduce(out=o, in_=i, op=ALU.add, axis=AX.X), [oh], [dst])
                    yield
                stt(idxf, idxf[:], i1s, i1s[:].rearrange("p h k -> p (h k)"), 128.0, i2s, i2s[:].rearrange("p h k -> p (h k)"), ALU.mult, ALU.add)
                cp("dve", idxu, idxu_u, idxf, idxf[:])
                yield
                tt("dve", gw, gw[:], sc, sc[:], sc, sc[:, :, 0:1].to_broadcast([128, 8, 16]), ALU.subtract)
                act(gw, gw[:], gw, gw[:], AF.Exp)
                yield
                op("dve", lambda e: e.tensor_reduce(out=gsum[:], in_=gw[:], op=ALU.add, axis=AX.X), [gw], [gsum])
                op("dve", lambda e: e.reciprocal(out=gsum[:], in_=gsum[:]), [gsum], [gsum])
                yield
                tt("dve", gw, gw[:], gw, gw[:], gsum, gsum[:].unsqueeze(2).to_broadcast([128, 8, 16]), ALU.mult)
                yield

            def advance(bg, n):
                for _ in range(n):
                    while bg:
                        try:
                            next(bg[0])
                            break
                        except StopIteration:
                            bg.pop(0)
                    if not bg:
                        return False
                return True

            def drain(bg):
                while bg:
                    for _ in bg[0]:
                        pass
                    bg.pop(0)

            def stageC(g0, s, bg, rate):
                xb = xt[2 * g0 + s]
                idxu = idxu2[s]; idxu_u = idxu2_u[s]; gw = gw2[s]; h2b = h2b2[s]
                gwf = gw[:].rearrange("p h k -> p (h k)")
                pacc = [psb[6], psb[7]]
                dve_hks = [hk for hk in range(128) if hk % PEMOD == 0] if PEMOD < 128 else []
                pe_hks = [hk for hk in range(128) if hk not in dve_hks]
                if dve_hks:
                    memset("dve", oth, oth[:], 0.0)
                for hk in range(128):
                    gb = GB[gctr[0] % NB]; gctr[0] += 1
                    r = hk % NR
                    fw.swdma(lambda e, o=gb[:], i=idxu_u[:, hk:hk + 1]: e.indirect_dma_start(out=o, out_offset=None, in_=puv_d[l], in_offset=bass.IndirectOffsetOnAxis(ap=i, axis=0)),
                             gb, extra_reads=[idxu])
                    stt(gb, gb[:, 0:1024], h2b, h2b[:], 1.0, gb, gb[:, 0:1024], ALU.mult, ALU.mult, accum=av_r[r][:, 0:1], accb=av_r[r])
                    act(wt_r[r], wt_r[r][:, 0:1], av_r[r], av_r[r][:, 0:1], AF.Gelu_apprx_tanh)
                    advance(bg, rate)
                    tt("dve", wv_r[r], wv_r[r][:, 0:1], wt_r[r], wt_r[r][:, 0:1], gw, gwf[:, hk:hk + 1], ALU.mult)
                    if hk in pe_hks:
                        dg = DG[dgc[0] % 4]; dgc[0] += 1
                        act(dg, dg[:], ident, ident[:], AF.Copy, scale=wv_r[r][:, 0:1], extra=[wv_r[r]])
                        for hf in range(2):
                            mm(pacc[hf], pacc[hf][:, :], dg, dg[:], gb, gb[:, 1024 + hf * 512:1024 + (hf + 1) * 512], hk == pe_hks[0], hk == pe_hks[-1])
                    else:
                        stt(oth, oth[:], gb, gb[:, 1024:2048], wv_r[r][:, 0:1], oth, oth[:], ALU.mult, ALU.add, extra=[wv_r[r]])
                for hf in range(2):
                    if dve_hks:
                        tt("dve", oth, oth[:, hf * 512:(hf + 1) * 512], oth, oth[:, hf * 512:(hf + 1) * 512], pacc[hf], pacc[hf][:, :], ALU.add)
                        tt("dve", oth, oth[:, hf * 512:(hf + 1) * 512], oth, oth[:, hf * 512:(hf + 1) * 512], gate_bc, gate_bc[:, hf * 512:(hf + 1) * 512], ALU.mult)
                    else:
                        tt("dve", oth, oth[:, hf * 512:(hf + 1) * 512], pacc[hf], pacc[hf][:, :], gate_bc, gate_bc[:, hf * 512:(hf + 1) * 512], ALU.mult)
                stt(oth, oth[:], xb, xb[:], ALPHA, oth, oth[:], ALU.mult, ALU.add)
                layer_norm(oth, oth[:], xb, xb[:], small)

            drain([stageA(0), stageB(0, 0)])
            for g0 in range(NG):
                for s in range(2):
                    if s == 0:
                        bg = [stageB(g0, 1)]; rate = RATE1
                    elif g0 + 1 < NG:
                        bg = [stageA(g0 + 1), stageB(g0 + 1, 0)]; rate = RATE2
                    else:
                        bg = []; rate = 0
                    stageC(g0, s, bg, rate)
                    drain(bg)

        try:
            for u in range(0, n_ctx, 2):
                run_unit(False, list(range(u, min(u + 2, n_ctx))))
            if do_sample:
                run_unit(True, 0)
        except StopIteration:
            pass
        fw.barrier()
        fw.run_block()
    return nc


def host_consts():
    ident = np.eye(128, dtype=np.float32)
    j = np.arange(128)[:, None]; i = np.arange(128)[None, :]
    same = (j // 64) == (i // 64)
    s = -1.0 / 16.0
    tri = np.zeros((128, 6, 128), np.float32)
    tri[:, 0, :] = np.where(same & (j <= i), s, 0.0)
    tri[:, 1, :] = np.where(same & (j >= i), s, 0.0)
    tri[:, 2, :] = np.where(same & (j > i), s, 0.0)
    tri[:, 3, :] = np.where(same & (j < i), s, 0.0)
    tri[:, 4, :] = np.where(same & (i >= j), 1.0, 0.0)
    tri[:, 5, :] = np.where(same & (i <= j), 1.0, 0.0)
    cind = np.zeros((128, 2), np.float32)
    cind[0:64, 0] = s; cind[64:128, 1] = s
    pos = np.arange(1024)
    r = (pos // 64).astype(np.float32); col = (pos % 64).astype(np.float32)
    inv = (np.float32(10000.0) ** (-np.arange(16, dtype=np.float32) / np.float32(16))).astype(np.float32)
    ang = np.concatenate([r[:, None] * inv[None, :], col[:, None] * inv[None, :]], axis=1).astype(np.float32)
    rope = np.zeros((128, 8, 2, 32), np.float32)
    angt = ang.reshape(8, 128, 32).transpose(1, 0, 2)
    rope[:, :, 0, :] = np.cos(angt); rope[:, :, 1, :] = np.sin(angt)
    iota = np.tile(np.arange(16, dtype=np.float32)[None, :], (128, 1))
    return {"c_ident": ident, "c_tri": tri, "c_cind": cind, "c_rope": rope, "c_iota": iota}


def make_in_maps(inp, n_cores=8):
    f = lambda a: np.ascontiguousarray(np.asarray(a, dtype=np.float32))
    consts = host_consts()
    ada_b = f(inp["ada_b"])
    ada_bT = np.ascontiguousarray(ada_b.reshape(2, 48, 128).transpose(2, 0, 1))
    gw2 = f(inp["gate_w2"]); gb = f(inp["gate_b"])
    gw2b = np.zeros((2, 17, 2, 512), np.float32)
    gw2b[:, 0:16, :, :] = gw2.transpose(0, 2, 1, 3)
    gw2b[:, 16, :, :] = gb
    shared = {
        "ada_w": f(inp["ada_w"]), "ada_bT": ada_bT, "w_in": f(inp["w_in"]), "q_norm": f(inp["q_norm"]), "k_norm": f(inp["k_norm"]),
        "gw2b_c": gw2b, "gla_norm": f(inp["gla_norm"]), "w_attn_o": f(inp["w_attn_o"]), "w_gla_o": f(inp["w_gla_o"]),
        "w_out": f(inp["w_out"]), "ln1_g": f(inp["ln1_g"]), "ln1_b": f(inp["ln1_b"]), "ln2_g": f(inp["ln2_g"]), "ln2_b": f(inp["ln2_b"]),
        "peer_wq": f(inp["peer_wq"]), "peer_sub_keys": f(inp["peer_sub_keys"]),
        "puv0": np.ascontiguousarray(np.concatenate([f(inp["peer_u"][0]), f(inp["peer_v"][0])], axis=1)),
        "puv1": np.ascontiguousarray(np.concatenate([f(inp["peer_u"][1]), f(inp["peer_v"][1])], axis=1)),
    }
    shared.update(consts)
    w_in_full = shared["w_in"]
    wglr = np.ascontiguousarray(w_in_full[:, :, 4608:4640])
    shared["wglr_c"] = wglr
    gw2b_rev = np.ascontiguousarray(gw2b[:, :, ::-1, :])
    wglr_rev = np.ascontiguousarray(np.concatenate([wglr[:, :, 16:32], wglr[:, :, 0:16]], axis=2))
    rope_n = consts["c_rope"]
    rope_rev = np.ascontiguousarray(rope_n[::-1, ::-1, :, :])
    xp = f(inp["x_prompt"]); xs = f(inp["x_sample"])
    ck = f(inp["cache_k"]).reshape(4, 2, 512, 256); cv = f(inp["cache_v"]).reshape(4, 2, 512, 256)
    sg = f(inp["state_gla"]); c = f(inp["c"]); cctx = f(inp["c_ctx"])
    maps = []
    for core in range(n_cores):
        b = core % 4
        cond = np.stack([cctx, c[b]], axis=1)
        condT = np.ascontiguousarray(cond.reshape(8, 128, 2).transpose(1, 0, 2))
        m = dict(shared)
        rev = core >= 4
        m.update({"xc": np.ascontiguousarray(xp[core * 4:(core + 1) * 4]),
                  "xs": np.ascontiguousarray(xs[b][::-1] if rev else xs[b]),
                  "ck": np.ascontiguousarray(ck[b]), "cv": np.ascontiguousarray(cv[b]),
                  "sg": np.ascontiguousarray(sg[b][:, ::-1] if rev else sg[b]),
                  "gw2b_s": gw2b_rev if rev else gw2b, "wglr_s": wglr_rev if rev else wglr,
                  "c_rope": rope_rev if rev else rope_n,
                  "condT": condT})
        maps.append(m)
    return maps


_NC_CACHE = {}


def kernel(**inputs):
    if "nc" not in _NC_CACHE:
        _NC_CACHE["nc"] = build_program()
    nc = _NC_CACHE["nc"]
    maps = make_in_maps(inputs)
    res = run_bass_kernel_spmd(nc, maps, core_ids=list(range(8)))
    R = res.results
    y_prompt = np.concatenate([R[i]["yc"] for i in range(8)], axis=0).astype(np.float32)
    y_sample = np.stack([np.concatenate([R[i]["ys"], R[i + 4]["ys"][::-1]], axis=0) for i in range(4)], axis=0).astype(np.float32)
    nk = np.concatenate([R[i]["nk"] for i in range(8)], axis=0).reshape(32, 2, 256, 4, 64).astype(np.float32)
    nv = np.concatenate([R[i]["nv"] for i in range(8)], axis=0).reshape(32, 2, 256, 4, 64).astype(np.float32)
    ns = np.concatenate([R[i]["ns"] for i in range(8)], axis=0).astype(np.float32)
    return (y_prompt, y_sample, nk, nv, ns)
```
